# Optimizing a Trainium2 kernel written in Bass

```python
import jax, jax.numpy as jnp
from jax import lax
import numpy as np

D_MODEL = 4096
BATCH = 2
SEQ = 4096
DEPTH = 2

GRID_W = 64
CTX_LEN = 256
N_MIXERS = 2
N_FOURIER_LAYERS = (DEPTH + 1) // 2
N_MLA_LAYERS = DEPTH // 2
FOURIER_GROUPS = 4
FOURIER_GROUP_DIM = D_MODEL // FOURIER_GROUPS
MLA_HEADS = 64
Q_LORA = 1536
KV_LORA = 512
QK_NOPE = 128
QK_ROPE = 64
V_HEAD = 128
QK_HEAD = QK_NOPE + QK_ROPE
SOFTMAX_SCALE = QK_HEAD ** -0.5
ROPE_BASE = 10000.0
Q_BLOCK = 128
FFN_HIDDEN = ((8 * D_MODEL + 3 * 256 - 1) // (3 * 256)) * 256
N_MOD = 6
EPS = 1e-6
ADA_INIT_GAIN = 0.5

kernel_name = "fnet_mla_interleaved_dit_block"


def rms_norm(t, g):
    tf = t.astype(jnp.float32)
    y = tf * lax.rsqrt(jnp.mean(tf * tf, axis=-1, keepdims=True) + EPS)
    return y.astype(t.dtype) * g


def modulate(h, shift, scale):
    return h * (1 + scale) + shift


def swiglu(h, w_gate, w_up, w_down):
    return (jax.nn.silu(h @ w_gate) * (h @ w_up)) @ w_down


def axial_rope_tables(n, dtype):
    rows_n = n // GRID_W
    row = jnp.repeat(jnp.arange(rows_n, dtype=jnp.float32), GRID_W)
    col = jnp.tile(jnp.arange(GRID_W, dtype=jnp.float32), rows_n)
    n_freq = QK_ROPE // 4
    inv_freq = ROPE_BASE ** (-jnp.arange(n_freq, dtype=jnp.float32) / n_freq)
    ang = jnp.concatenate([row[:, None] * inv_freq, col[:, None] * inv_freq], axis=-1)
    return jnp.cos(ang).astype(dtype), jnp.sin(ang).astype(dtype)


def apply_axial_rope(t, cos, sin):
    q = QK_ROPE // 4
    bshape = (t.shape[1],) + (1,) * (t.ndim - 3) + (2 * q,)
    cos = cos.reshape(bshape)
    sin = sin.reshape(bshape)
    cr, sr, cc, sc = cos[..., :q], sin[..., :q], cos[..., q:], sin[..., q:]
    r1, r2, c1, c2 = t[..., :q], t[..., q:2 * q], t[..., 2 * q:3 * q], t[..., 3 * q:]
    return jnp.concatenate([r1 * cr - r2 * sr, r2 * cr + r1 * sr,
                            c1 * cc - c2 * sc, c2 * cc + c1 * sc], axis=-1)


def fourier_2d(h):
    b, n, _ = h.shape
    u = h.astype(jnp.float32).reshape(b, n, FOURIER_GROUPS, FOURIER_GROUP_DIM)
    f = jnp.fft.fft2(u, axes=(1, 3), norm="ortho").real
    return f.reshape(b, n, D_MODEL).astype(h.dtype)


def fourier_mix(h, hc, w_out, need_ctx):
    y = fourier_2d(h) @ w_out
    yc = fourier_2d(hc) @ w_out if need_ctx else None
    return y, yc


def mla_queries(c_q, g_q, w_uq, rope):
    b, n, _ = c_q.shape
    q = (rms_norm(c_q, g_q) @ w_uq).reshape(b, n, MLA_HEADS, QK_HEAD)
    q_nope, q_rope = q[..., :QK_NOPE], q[..., QK_NOPE:]
    if rope is not None:
        q_rope = apply_axial_rope(q_rope, *rope)
    return q_nope, q_rope


def mla_keys_values(c_kv, k_rope, g_kv, w_ukv, rope):
    b, n, _ = c_kv.shape
    kv = (rms_norm(c_kv, g_kv) @ w_ukv).reshape(b, n, MLA_HEADS, QK_NOPE + V_HEAD)
    k_nope, v = kv[..., :QK_NOPE], kv[..., QK_NOPE:]
    if rope is not None:
        k_rope = apply_axial_rope(k_rope, *rope)
    return k_nope, k_rope, v


def mla_attend(q_nope, q_rope, k_nope, k_rope, v):
    s = (jnp.einsum('bqhd,bkhd->bhqk', q_nope, k_nope)
         + jnp.einsum('bqhr,bkr->bhqk', q_rope, k_rope))
    p = jax.nn.softmax(s.astype(jnp.float32) * SOFTMAX_SCALE, axis=-1).astype(v.dtype)
    return jnp.einsum('bhqk,bkhd->bqhd', p, v)


def mla_mix(h, hc, w_a, g_q, g_kv, w_uq, w_ukv, w_o, rope, need_ctx):
    b, n, _ = h.shape
    nc = hc.shape[1]
    cq_l, ckv_l, kr_l = jnp.split(h @ w_a, [Q_LORA, Q_LORA + KV_LORA], axis=-1)
    ql_n, ql_r = mla_queries(cq_l, g_q, w_uq, rope)
    kl_n, kl_r, vl = mla_keys_values(ckv_l, kr_l, g_kv, w_ukv, rope)
    if need_ctx:
        cq_c, ckv_c, kr_c = jnp.split(hc @ w_a, [Q_LORA, Q_LORA + KV_LORA], axis=-1)
    else:
        ckv_c, kr_c = jnp.split(hc @ w_a[:, Q_LORA:], [KV_LORA], axis=-1)
    kc_n, kc_r, vc = mla_keys_values(ckv_c, kr_c, g_kv, w_ukv, None)
    k_n = jnp.concatenate([kc_n, kl_n], axis=1)
    k_r = jnp.concatenate([kc_r, kl_r], axis=1)
    v_all = jnp.concatenate([vc, vl], axis=1)
    nb = n // Q_BLOCK

    def to_blocks(t):
        return t.reshape((b, nb, Q_BLOCK) + t.shape[2:]).swapaxes(0, 1)

    o = lax.map(lambda qs: mla_attend(qs[0], qs[1], k_n, k_r, v_all), (to_blocks(ql_n), to_blocks(ql_r)))
    o = o.swapaxes(0, 1).reshape(b, n, MLA_HEADS * V_HEAD)
    y = o @ w_o
    yc = None
    if need_ctx:
        qc_n, qc_r = mla_queries(cq_c, g_q, w_uq, None)
        oc = mla_attend(qc_n, qc_r, kc_n, kc_r, vc).reshape(b, nc, MLA_HEADS * V_HEAD)
        yc = oc @ w_o
    return y, yc


def setup_inputs(seed: int = 0) -> dict:
    key = jax.random.key(seed)
    ks = jax.random.split(key, 20)
    f32 = jnp.float32

    def nrm(k, shape, scale):
        return jax.random.normal(k, shape, f32) * scale

    hid_q = MLA_HEADS * QK_HEAD
    hid_kv = MLA_HEADS * (QK_NOPE + V_HEAD)
    hid_o = MLA_HEADS * V_HEAD
    return {
        "x": nrm(ks[0], (BATCH, SEQ, D_MODEL), 1.0),
        "c": nrm(ks[1], (BATCH, D_MODEL), 1.0),
        "ctx": nrm(ks[2], (BATCH, CTX_LEN, D_MODEL), 1.0),
        "c_ctx": nrm(ks[3], (D_MODEL,), 1.0),
        "w_ada": nrm(ks[4], (DEPTH, D_MODEL, N_MOD * D_MODEL), ADA_INIT_GAIN * D_MODEL ** -0.5),
        "b_ada": nrm(ks[5], (DEPTH, N_MOD * D_MODEL), 0.02),
        "g_mix": 1.0 + nrm(ks[6], (DEPTH, D_MODEL), 0.1),
        "g_ffn": 1.0 + nrm(ks[7], (DEPTH, D_MODEL), 0.1),
        "fourier_w_out": nrm(ks[8], (N_FOURIER_LAYERS, D_MODEL, D_MODEL), D_MODEL ** -0.5),
        "mla_w_a": nrm(ks[9], (N_MLA_LAYERS, D_MODEL, Q_LORA + KV_LORA + QK_ROPE), D_MODEL ** -0.5),
        "mla_g_q": 1.0 + nrm(ks[10], (N_MLA_LAYERS, Q_LORA), 0.1),
        "mla_g_kv": 1.0 + nrm(ks[11], (N_MLA_LAYERS, KV_LORA), 0.1),
        "mla_w_uq": nrm(ks[12], (N_MLA_LAYERS, Q_LORA, hid_q), Q_LORA ** -0.5),
        "mla_w_ukv": nrm(ks[13], (N_MLA_LAYERS, KV_LORA, hid_kv), KV_LORA ** -0.5),
        "mla_w_o": nrm(ks[14], (N_MLA_LAYERS, hid_o, D_MODEL), hid_o ** -0.5),
        "w_gate": nrm(ks[15], (DEPTH, D_MODEL, FFN_HIDDEN), D_MODEL ** -0.5),
        "w_up": nrm(ks[16], (DEPTH, D_MODEL, FFN_HIDDEN), D_MODEL ** -0.5),
        "w_down": nrm(ks[17], (DEPTH, FFN_HIDDEN, D_MODEL), FFN_HIDDEN ** -0.5),
        "g_final": 1.0 + nrm(ks[18], (D_MODEL,), 0.1),
    }


def reference(x, c, ctx, c_ctx, w_ada, b_ada, g_mix, g_ffn, fourier_w_out, mla_w_a, mla_g_q, mla_g_kv,
              mla_w_uq, mla_w_ukv, mla_w_o, w_gate, w_up, w_down, g_final):
    rope = axial_rope_tables(x.shape[1], x.dtype)
    xc = ctx
    for i in range(DEPTH):
        need_ctx = i < DEPTH - 1
        j = i // N_MIXERS
        mod = jax.nn.silu(c) @ w_ada[i] + b_ada[i]
        mod_c = jax.nn.silu(c_ctx) @ w_ada[i] + b_ada[i]
        sh_m, sc_m, gt_m, sh_f, sc_f, gt_f = jnp.split(mod[:, None, :], N_MOD, axis=-1)
        csh_m, csc_m, cgt_m, csh_f, csc_f, cgt_f = jnp.split(mod_c, N_MOD, axis=-1)
        h = modulate(rms_norm(x, g_mix[i]), sh_m, sc_m)
        hc = modulate(rms_norm(xc, g_mix[i]), csh_m, csc_m)
        if i % N_MIXERS == 0:
            y, yc = fourier_mix(h, hc, fourier_w_out[j], need_ctx)
        else:
            y, yc = mla_mix(h, hc, mla_w_a[j], mla_g_q[j], mla_g_kv[j], mla_w_uq[j], mla_w_ukv[j],
                            mla_w_o[j], rope, need_ctx)
        x = x + gt_m * y
        x = x + gt_f * swiglu(modulate(rms_norm(x, g_ffn[i]), sh_f, sc_f), w_gate[i], w_up[i], w_down[i])
        if need_ctx:
            xc = xc + cgt_m * yc
            xc = xc + cgt_f * swiglu(modulate(rms_norm(xc, g_ffn[i]), csh_f, csc_f),
                                     w_gate[i], w_up[i], w_down[i])
    return rms_norm(x, g_final)
```

```python
import numpy as np
import ml_dtypes
from contextlib import ExitStack
import concourse.bass as bass
import concourse.mybir as mybir
from concourse.bass_utils import run_bass_kernel_spmd

F32 = mybir.dt.float32
BF16 = mybir.dt.bfloat16
AF = mybir.ActivationFunctionType
ALU = mybir.AluOpType
AX = mybir.AxisListType
bf16 = ml_dtypes.bfloat16
NCORES = 8
EPS = 1e-6
NOPE = 128
ROPE = 64
VH = 128
QKH = NOPE + ROPE
SCALE = QKH ** -0.5

CFG_FULL = dict(D=4096, SEQ=4096, CTX=256, HEADS=64, QL=1536, KVL=512, FFN=11008, GRID_W=64, BATCH=2)


class Res:
    def __init__(self, name):
        self.name = name
        self.last_w = None
        self.readers = []


class Op:
    __slots__ = ("eng", "fn", "deps", "dma", "sem", "val", "signal", "key", "grp")

    def __init__(self, eng, fn, dma, key, grp):
        self.eng = eng
        self.fn = fn
        self.deps = []
        self.dma = dma
        self.sem = None
        self.val = 0
        self.signal = dma
        self.key = key
        self.grp = grp


ENGS = ("pe", "act", "dve", "pool", "sp")
CENGS = ("pe", "act", "dve", "pool")
NGRP = 4
ARENA = 102400


class Prog:
    def __init__(self, nc, stack):
        self.nc = nc
        self.stack = stack
        self.ops = {e: [] for e in ENGS}
        self.eng_sem = {(e, g): stack.enter_context(nc.semaphore(f"cs_{e}{g}")) for e in CENGS for g in range(NGRP)}
        self.dma_sems = {}
        self.dma_cnt = {}
        self.out_dmas = []
        self.stage = 0
        self.pending = {e: None for e in ENGS}
        self.stage_dmas = []
        self.stage_keys = {}
        self.arena = stack.enter_context(nc.sbuf_tensor("arena", [128, ARENA], BF16))
        self.off = 0

    def alloc(self, name, shape, dt):
        per = 1
        for s in shape[1:]:
            per *= s
        nb = per * (4 if dt == F32 else 2)
        nb = (nb + 63) // 64 * 64
        ne = nb // 2
        assert self.off + ne <= ARENA, (name, shape, self.off)
        v = self.arena[0:shape[0], self.off:self.off + per * (2 if dt == F32 else 1)]
        self.off += ne
        if dt == F32:
            v = v.bitcast(F32)
        if len(shape) == 3:
            v = v.rearrange("p (a b) -> p a b", b=shape[2])
        return v, Res(name)

    def barrier(self):
        deps = []
        for e in CENGS:
            for op in reversed(self.ops[e]):
                if not op.dma:
                    deps.append(op)
                    break
        deps.extend(self.stage_dmas)
        self.stage_dmas = []
        for e in ENGS:
            prev = self.pending[e] or []
            self.pending[e] = prev + deps
        self.stage += 1
        self.off = 0
        self.stage_keys = {}

    def _add(self, eng, fn, reads, writes, dma=False, key=None, chain=False):
        grp = self.stage % NGRP
        if key is not None:
            if key not in self.stage_keys:
                self.stage_keys[key] = len(self.stage_keys)
            key = f"{grp}_{self.stage_keys[key]}"
        op = Op(eng, fn, dma, key, grp)
        deps = []
        for r in reads:
            if r.last_w is not None:
                deps.append(r.last_w)
        for w in writes:
            if w.last_w is not None:
                if not (chain and w.last_w.dma and w.last_w.key == key):
                    deps.append(w.last_w)
                else:
                    deps.extend(w.last_w.deps)
            deps.extend(w.readers)
        if self.pending[eng]:
            deps.extend(self.pending[eng])
            self.pending[eng] = None
        seen = set()
        for d in deps:
            if id(d) not in seen and d is not op:
                seen.add(id(d))
                op.deps.append(d)
        for r in reads:
            r.readers.append(op)
        for w in writes:
            w.last_w = op
            w.readers = []
        if dma:
            if key not in self.dma_sems:
                self.dma_sems[key] = self.stack.enter_context(self.nc.semaphore("ds_" + key))
                self.dma_cnt[key] = 0
            self.dma_cnt[key] += 16
            op.sem = self.dma_sems[key]
            op.val = self.dma_cnt[key]
            self.stage_dmas.append(op)
        self.ops[eng].append(op)
        return op

    def op(self, eng, fn, reads=(), writes=()):
        return self._add(eng, fn, reads, writes)

    def dma(self, eng, key, out, in_, reads=(), writes=(), chain=False, is_out=False):
        op = self._add(eng, lambda e, out=out, in_=in_: e.dma_start(out=out, in_=in_), reads, writes,
                       dma=True, key=key, chain=chain)
        if is_out:
            self.out_dmas.append(op)
        return op

    def emit(self):
        fin = Op("sp", None, False, None, 0)
        fin.deps = list(self.out_dmas)
        self.ops["sp"].append(fin)
        for e in ENGS:
            for op in self.ops[e]:
                for d in op.deps:
                    if not d.dma:
                        if d.eng == "pe" and op.eng == "pe" and not op.dma:
                            continue
                        d.signal = True
        for e in CENGS:
            cnt = [0] * NGRP
            for op in self.ops[e]:
                if not op.dma and op.signal:
                    cnt[op.grp] += 1
                    op.sem = self.eng_sem[(e, op.grp)]
                    op.val = cnt[op.grp]
        nc = self.nc
        ops = self.ops

        def run(e, eng):
            seen = {}
            for op in ops[e]:
                need = {}
                for d in op.deps:
                    if not d.dma and d.eng == "pe" and e == "pe" and not op.dma:
                        continue
                    k = id(d.sem)
                    if seen.get(k, 0) >= d.val:
                        continue
                    if k not in need or need[k][1] < d.val:
                        need[k] = (d.sem, d.val)
                for k, (s, v) in need.items():
                    eng.wait_ge(s, v)
                    seen[k] = v
                if op.fn is None:
                    continue
                ins = op.fn(eng)
                if op.signal:
                    ins.then_inc(op.sem, 16 if op.dma else 1)

        with nc.Block() as block:
            @block.tensor
            def _(eng):
                run("pe", eng)

            @block.scalar
            def _(eng):
                run("act", eng)

            @block.vector
            def _(eng):
                run("dve", eng)

            @block.gpsimd
            def _(eng):
                run("pool", eng)

            @block.sync
            def _(eng):
                run("sp", eng)


def _nblocks(n, nb=512):
    return [(i, min(nb, n - i)) for i in range(0, n, nb)]


class PS:
    def __init__(self, nc, st):
        self.f = [st.enter_context(nc.psum_tensor(f"psf{i}", [128, 512], F32)) for i in range(5)]
        self.fr = [Res(f"psf{i}") for i in range(5)]
        self.t = [st.enter_context(nc.psum_tensor(f"pst{i}", [128, 4, 128], BF16)) for i in range(3)]
        self.tr = [Res(f"pst{i}") for i in range(3)]


def st_gemm(p, ps, at, MT, KC, bs, N, epi, out, r_ap=None, grow=None, brow=None):
    odt = out.dtype
    nB = len(bs)
    NB = 512
    bv = [b.rearrange("(kc p) n -> p kc n", p=128) for b in bs]
    nbuf = 2 if (KC * NB * 2 * nB * 2 <= 110 * 1024) else 1
    bsb = [[p.alloc(f"bsb{i}_{j}", [128, KC, NB], BF16) for i in range(nB)] for j in range(nbuf)]
    att = [p.alloc(f"at{i}", [128, KC, 128], BF16) for i in range(2)]
    ot = [p.alloc(f"ot{i}", [128, NB], odt) for i in range(2)]
    if epi != "plain":
        tmp = [p.alloc(f"tmp{i}", [128, NB], F32) for i in range(2)]
    if epi == "resid":
        rt = [p.alloc(f"rt{i}", [128, NB], F32) for i in range(2)]
        gt = [p.alloc(f"gt{i}", [128, NB], F32) for i in range(2)]
    if epi == "bias":
        bt = [p.alloc(f"bt{i}", [128, NB], F32) for i in range(2)]
    NPS = 4 // nB
    it = 0
    KG = 8
    for bi, (n0, nb) in enumerate(_nblocks(N, NB)):
        bb = bsb[bi % nbuf]
        for i in range(nB):
            for k0 in range(0, KC, KG):
                k1 = min(KC, k0 + KG)
                p.dma("pool", f"b{i}_{bi % nbuf}", bb[i][0][:, k0:k1, 0:nb], bv[i][:, k0:k1, n0:n0 + nb],
                      writes=[bb[i][1]], chain=(k0 > 0))
        if epi == "bias":
            p.dma("pool", f"bt{bi % 2}", bt[bi % 2][0][:, 0:nb], brow[0:1, n0:n0 + nb].partition_broadcast(128), writes=[bt[bi % 2][1]])
        for mt in range(MT):
            s = it % 2
            q = it % NPS
            it += 1
            p.dma("sp", f"at{s}", att[s][0][:, :, :], at[mt], writes=[att[s][1]])
            if epi == "resid":
                p.dma("act", f"rt{s}", rt[s][0][:, 0:nb], r_ap[mt * 128:(mt + 1) * 128, n0:n0 + nb], writes=[rt[s][1]])
                p.dma("act", f"gt{s}", gt[s][0][:, 0:nb], grow(mt)[0:1, n0:n0 + nb].partition_broadcast(128), writes=[gt[s][1]])
            for i in range(nB):
                z = q * nB + i

                def mm(e, i=i, s=s, z=z, nb=nb, bb=bb):
                    ins = None
                    for kc in range(KC):
                        ins = e.matmul(ps.f[z][:, 0:nb], att[s][0][:, kc, :], bb[i][0][:, kc, 0:nb],
                                       start=(kc == 0), stop=(kc == KC - 1))
                    return ins
                p.op("pe", mm, reads=[att[s][1], bb[i][1]], writes=[ps.fr[z]])
            z0 = q * nB
            if epi == "plain":
                if it % 2 == 0:
                    p.op("act", lambda e, s=s, z0=z0, nb=nb: e.activation(out=ot[s][0][:, 0:nb], in_=ps.f[z0][:, 0:nb], func=AF.Copy),
                         reads=[ps.fr[z0]], writes=[ot[s][1]])
                else:
                    p.op("dve", lambda e, s=s, z0=z0, nb=nb: e.tensor_copy(out=ot[s][0][:, 0:nb], in_=ps.f[z0][:, 0:nb]),
                         reads=[ps.fr[z0]], writes=[ot[s][1]])
            elif epi == "silu":
                p.op("act", lambda e, s=s, z0=z0, nb=nb: e.activation(out=tmp[s][0][:, 0:nb], in_=ps.f[z0][:, 0:nb], func=AF.Silu),
                     reads=[ps.fr[z0]], writes=[tmp[s][1]])
                p.op("dve", lambda e, s=s, z0=z0, nb=nb: e.tensor_tensor(out=ot[s][0][:, 0:nb], in0=tmp[s][0][:, 0:nb], in1=ps.f[z0 + 1][:, 0:nb], op=ALU.mult),
                     reads=[tmp[s][1], ps.fr[z0 + 1]], writes=[ot[s][1]])
            elif epi == "resid":
                p.op("dve", lambda e, s=s, z0=z0, nb=nb: e.tensor_tensor(out=tmp[s][0][:, 0:nb], in0=ps.f[z0][:, 0:nb], in1=gt[s][0][:, 0:nb], op=ALU.mult),
                     reads=[ps.fr[z0], gt[s][1]], writes=[tmp[s][1]])
                p.op("pool", lambda e, s=s, nb=nb: e.tensor_tensor(out=ot[s][0][:, 0:nb], in0=tmp[s][0][:, 0:nb], in1=rt[s][0][:, 0:nb], op=ALU.add),
                     reads=[tmp[s][1], rt[s][1]], writes=[ot[s][1]])
            else:
                p.op("dve", lambda e, s=s, z0=z0, nb=nb, bi=bi: e.tensor_tensor(out=ot[s][0][:, 0:nb], in0=ps.f[z0][:, 0:nb], in1=bt[bi % 2][0][:, 0:nb], op=ALU.add),
                     reads=[ps.fr[z0], bt[bi % 2][1]], writes=[ot[s][1]])
            p.dma("sp", f"ot{s}", out[mt * 128:(mt + 1) * 128, n0:n0 + nb], ot[s][0][:, 0:nb], reads=[ot[s][1]])
    p.barrier()


def st_trans(p, ps, ident, src_fn, MT, K, at_out):
    KC = K // 128
    xt = [p.alloc(f"xt{i}", [128, K], BF16) for i in range(2)]
    att = [p.alloc(f"att{i}", [128, KC, 128], BF16) for i in range(2)]
    iT = 0
    for mt in range(MT):
        s = mt % 2
        for j, (c0, ap) in enumerate(src_fn(mt)):
            w = ap.shape[1]
            p.dma("sp" if j % 2 == 0 else "act", f"xt{s}", xt[s][0][:, c0:c0 + w], ap, writes=[xt[s][1]], chain=(j > 0))
        for c0 in range(0, KC, 4):
            c1 = min(KC, c0 + 4)
            z = iT % 3
            iT += 1

            def tr(e, s=s, z=z, c0=c0, c1=c1):
                ins = None
                for c in range(c0, c1):
                    ins = e.transpose(out=ps.t[z][:, c - c0, :], in_=xt[s][0][:, c * 128:(c + 1) * 128], identity=ident[0][:, :])
                return ins
            p.op("pe", tr, reads=[xt[s][1], ident[1]], writes=[ps.tr[z]])
            if (c0 // 4) % 2 == 0:
                p.op("dve", lambda e, s=s, z=z, c0=c0, c1=c1: e.tensor_copy(out=att[s][0][:, c0:c1, :], in_=ps.t[z][:, 0:c1 - c0, :]),
                     reads=[ps.tr[z]], writes=[att[s][1]])
            else:
                p.op("act", lambda e, s=s, z=z, c0=c0, c1=c1: e.activation(out=att[s][0][:, c0:c1, :], in_=ps.t[z][:, 0:c1 - c0, :], func=AF.Copy),
                     reads=[ps.tr[z]], writes=[att[s][1]])
        p.dma("pool", f"att{s}", at_out[mt], att[s][0][:, :, :], reads=[att[s][1]])
    p.barrier()


def st_norm(p, x_fn, NT, D, g_row, sc_fn, sh_fn, out_fn, out_dt, final=False):
    gb = p.alloc("gb", [128, D], F32)
    xt = [p.alloc(f"xt{i}", [128, D], F32) for i in range(2)]
    sct = [p.alloc(f"sct{i}", [128, D], F32) for i in range(2)]
    sht = [p.alloc(f"sht{i}", [128, D], F32) for i in range(2)]
    sq = p.alloc("sq", [128, D], F32)
    ss = [p.alloc(f"ss{i}", [128, 4], F32) for i in range(2)]
    ot = [p.alloc(f"ot{i}", [128, D], out_dt) for i in range(2)]
    p.dma("pool", "gb", gb[0][:, :], g_row.partition_broadcast(128), writes=[gb[1]])
    for t in range(NT):
        s = t % 2
        p.dma("sp", f"xt{s}", xt[s][0][:, :], x_fn(t), writes=[xt[s][1]])
        p.dma("pool", f"sct{s}", sct[s][0][:, :], sc_fn(t).partition_broadcast(128), writes=[sct[s][1]])
        p.dma("pool", f"sht{s}", sht[s][0][:, :], sh_fn(t).partition_broadcast(128), writes=[sht[s][1]])
        p.op("act", lambda e, s=s: e.activation(out=sq[0][:, :], in_=xt[s][0][:, :], func=AF.Square, accum_out=ss[s][0][:, 0:1]),
             reads=[xt[s][1]], writes=[sq[1], ss[s][1]])
        p.op("dve", lambda e, s=s: e.tensor_scalar(out=ss[s][0][:, 1:2], in0=ss[s][0][:, 0:1], scalar1=1.0 / D, scalar2=EPS, op0=ALU.mult, op1=ALU.add),
             reads=[ss[s][1]], writes=[ss[s][1]])
        p.op("act", lambda e, s=s: e.activation(out=ss[s][0][:, 2:3], in_=ss[s][0][:, 1:2], func=AF.Sqrt),
             reads=[ss[s][1]], writes=[ss[s][1]])
        p.op("dve", lambda e, s=s: e.reciprocal(out=ss[s][0][:, 3:4], in_=ss[s][0][:, 2:3]),
             reads=[ss[s][1]], writes=[ss[s][1]])
        p.op("dve", lambda e, s=s: e.scalar_tensor_tensor(out=sct[s][0][:, :], in0=sct[s][0][:, :], scalar=1.0, in1=gb[0][:, :], op0=ALU.add, op1=ALU.mult),
             reads=[sct[s][1], gb[1]], writes=[sct[s][1]])
        p.op("dve", lambda e, s=s: e.scalar_tensor_tensor(out=xt[s][0][:, :], in0=xt[s][0][:, :], scalar=ss[s][0][:, 3:4], in1=sct[s][0][:, :], op0=ALU.mult, op1=ALU.mult),
             reads=[xt[s][1], ss[s][1], sct[s][1]], writes=[xt[s][1]])
        p.op("pool", lambda e, s=s: e.tensor_tensor(out=ot[s][0][:, :], in0=xt[s][0][:, :], in1=sht[s][0][:, :], op=ALU.add),
             reads=[xt[s][1], sht[s][1]], writes=[ot[s][1]])
        p.dma("sp", f"ot{s}", out_fn(t), ot[s][0][:, :], reads=[ot[s][1]], is_out=final)
    p.barrier()


def st_silu(p, a_ap, out_ap, Fd):
    at = p.alloc("sa", [128, Fd], F32)
    ot = p.alloc("so", [128, Fd], BF16)
    p.dma("sp", "sa", at[0][:, :], a_ap, writes=[at[1]])
    p.op("act", lambda e: e.activation(out=ot[0][:, :], in_=at[0][:, :], func=AF.Silu), reads=[at[1]], writes=[ot[1]])
    p.dma("sp", "so", out_ap, ot[0][:, :], reads=[ot[1]])
    p.barrier()


def st_muladd(p, NT, shape, a_dt, a_fn, b_fn, c_fn, d_fn, out_fn):
    dts = [a_dt, F32, a_dt, F32]
    fns = [a_fn, b_fn, c_fn, d_fn]
    names = "abcd"
    tl = [[p.alloc(f"{names[j]}{i}", shape, dts[j]) for i in range(2)] for j in range(4)]
    t1 = [p.alloc(f"t1{i}", shape, F32) for i in range(2)]
    t2 = [p.alloc(f"t2{i}", shape, F32) for i in range(2)]
    ot = [p.alloc(f"ot{i}", shape, BF16) for i in range(2)]
    for t in range(NT):
        s = t % 2
        for j in range(4):
            p.dma("sp" if j % 2 == 0 else "pool", f"{names[j]}{s}", tl[j][s][0], fns[j](t), writes=[tl[j][s][1]])
        p.op("dve", lambda e, s=s: e.tensor_tensor(out=t1[s][0], in0=tl[0][s][0], in1=tl[1][s][0], op=ALU.mult),
             reads=[tl[0][s][1], tl[1][s][1]], writes=[t1[s][1]])
        p.op("pool", lambda e, s=s: e.tensor_tensor(out=t2[s][0], in0=tl[2][s][0], in1=tl[3][s][0], op=ALU.mult),
             reads=[tl[2][s][1], tl[3][s][1]], writes=[t2[s][1]])
        p.op("dve", lambda e, s=s: e.tensor_tensor(out=ot[s][0], in0=t1[s][0], in1=t2[s][0], op=ALU.add),
             reads=[t1[s][1], t2[s][1]], writes=[ot[s][1]])
        p.dma("sp", f"ot{s}", out_fn(t), ot[s][0], reads=[ot[s][1]])
    p.barrier()


def st_attn(p, ps, ident, H, NQ, NK, qx, qrot, kvx, krot, o):
    NKC = NK // 128
    NQT = NQ // 128
    kbl = _nblocks(NK, 512)
    krtok = p.alloc("krtok", [128, NKC, ROPE], BF16)
    krT = p.alloc("krT", [64, NK], BF16)
    qtok = [p.alloc(f"qtok{i}", [128, NQT, NOPE], BF16) for i in range(2)]
    qrtok = [p.alloc(f"qrtok{i}", [128, NQT, ROPE], BF16) for i in range(2)]
    ktok = [p.alloc(f"ktok{i}", [128, NKC, NOPE], BF16) for i in range(2)]
    vt = [p.alloc(f"vt{i}", [128, NKC, VH], BF16) for i in range(2)]
    qn = [p.alloc(f"qn{i}", [128, NQ], BF16) for i in range(2)]
    qr_ = [p.alloc(f"qr{i}", [64, NQ], BF16) for i in range(2)]
    kn = [p.alloc(f"kn{i}", [128, NK], BF16) for i in range(2)]
    S = [p.alloc(f"S{i}", [128, NK], F32) for i in range(2)]
    P = [p.alloc(f"P{i}", [128, NK], BF16) for i in range(2)]
    PT = [p.alloc(f"PT{i}", [128, NKC, 128], BF16) for i in range(2)]
    st4 = [p.alloc(f"st{i}", [128, 4], F32) for i in range(2)]
    ot = [p.alloc(f"ot{i}", [128, 128], BF16) for i in range(2)]
    cnt = {"S": 0, "T": 0, "q": 0, "e": 0}

    def trans_into(src, dst, nchunk, rows_out):
        for c0 in range(0, nchunk, 4):
            c1 = min(nchunk, c0 + 4)
            z = cnt["T"] % 3
            cnt["T"] += 1

            def tr(e, z=z, c0=c0, c1=c1):
                ins = None
                for c in range(c0, c1):
                    ins = e.transpose(out=ps.t[z][0:rows_out, c - c0, :], in_=src[0][:, c, :], identity=ident[0][:, :])
                return ins
            p.op("pe", tr, reads=[src[1], ident[1]], writes=[ps.tr[z]])
            cnt["e"] += 1
            dv = dst[0][:, c0 * 128:c1 * 128].rearrange("p (a b) -> p a b", b=128)
            if cnt["e"] % 2 == 0:
                p.op("dve", lambda e, z=z, c0=c0, c1=c1, dv=dv: e.tensor_copy(out=dv, in_=ps.t[z][0:rows_out, 0:c1 - c0, :]),
                     reads=[ps.tr[z]], writes=[dst[1]])
            else:
                p.op("act", lambda e, z=z, c0=c0, c1=c1, dv=dv: e.activation(out=dv, in_=ps.t[z][0:rows_out, 0:c1 - c0, :], func=AF.Copy),
                     reads=[ps.tr[z]], writes=[dst[1]])

    p.dma("sp", "krtok", krtok[0][:, :, :], krot.rearrange("(c p) d -> p c d", p=128), writes=[krtok[1]])
    trans_into(krtok, krT, NKC, ROPE)
    it = 0
    for h in range(H):
        b = h % 2
        p.dma("sp", f"qtok{b}", qtok[b][0][:, :, :], qx[:, h * QKH:h * QKH + NOPE].rearrange("(c p) d -> p c d", p=128), writes=[qtok[b][1]])
        p.dma("sp", f"qrtok{b}", qrtok[b][0][:, :, :], qrot[:, h * ROPE:(h + 1) * ROPE].rearrange("(c p) d -> p c d", p=128), writes=[qrtok[b][1]])
        p.dma("pool", f"ktok{b}", ktok[b][0][:, :, :], kvx[:, h * 256:h * 256 + NOPE].rearrange("(c p) d -> p c d", p=128), writes=[ktok[b][1]])
        p.dma("act", f"vt{b}", vt[b][0][:, :, :], kvx[:, h * 256 + NOPE:(h + 1) * 256].rearrange("(c p) d -> p c d", p=128), writes=[vt[b][1]])
        trans_into(ktok[b], kn[b], NKC, 128)
        trans_into(qtok[b], qn[b], NQT, 128)
        trans_into(qrtok[b], qr_[b], NQT, ROPE)
        for qt in range(NQT):
            s = it % 2
            it += 1
            q0 = qt * 128
            for bi, (k0, kb) in enumerate(kbl):
                z = cnt["S"] % 3
                cnt["S"] += 1

                def mmS(e, b=b, z=z, q0=q0, k0=k0, kb=kb):
                    e.matmul(ps.f[z][:, 0:kb], qn[b][0][:, q0:q0 + 128], kn[b][0][:, k0:k0 + kb], start=True, stop=False)
                    return e.matmul(ps.f[z][:, 0:kb], qr_[b][0][:, q0:q0 + 128], krT[0][:, k0:k0 + kb], start=False, stop=True)
                p.op("pe", mmS, reads=[qn[b][1], qr_[b][1], kn[b][1], krT[1]], writes=[ps.fr[z]])
                if bi % 2 == 0:
                    p.op("dve", lambda e, s=s, z=z, k0=k0, kb=kb: e.tensor_copy(out=S[s][0][:, k0:k0 + kb], in_=ps.f[z][:, 0:kb]),
                         reads=[ps.fr[z]], writes=[S[s][1]])
                else:
                    p.op("act", lambda e, s=s, z=z, k0=k0, kb=kb: e.activation(out=S[s][0][:, k0:k0 + kb], in_=ps.f[z][:, 0:kb], func=AF.Copy),
                         reads=[ps.fr[z]], writes=[S[s][1]])
            p.op("dve", lambda e, s=s: e.reduce_max(out=st4[s][0][:, 0:1], in_=S[s][0][:, :], axis=AX.X),
                 reads=[S[s][1]], writes=[st4[s][1]])
            p.op("dve", lambda e, s=s: e.tensor_scalar(out=st4[s][0][:, 1:2], in0=st4[s][0][:, 0:1], scalar1=-SCALE, scalar2=None, op0=ALU.mult),
                 reads=[st4[s][1]], writes=[st4[s][1]])
            p.op("act", lambda e, s=s: e.activation(out=P[s][0][:, :], in_=S[s][0][:, :], func=AF.Exp, bias=st4[s][0][:, 1:2], scale=SCALE, accum_out=st4[s][0][:, 2:3]),
                 reads=[S[s][1], st4[s][1]], writes=[P[s][1], st4[s][1]])
            Pv = (P[s][0].rearrange("p (a b) -> p a b", b=128), P[s][1])
            PTv = (PT[s][0].rearrange("p a b -> p (a b)"), PT[s][1])
            trans_into(Pv, PTv, NKC, 128)
            zo = 3 + (it % 2)

            def mmO(e, s=s, b=b, zo=zo):
                ins = None
                for c in range(NKC):
                    ins = e.matmul(ps.f[zo][:, 0:128], PT[s][0][:, c, :], vt[b][0][:, c, :], start=(c == 0), stop=(c == NKC - 1))
                return ins
            p.op("pe", mmO, reads=[PT[s][1], vt[b][1]], writes=[ps.fr[zo]])
            p.op("dve", lambda e, s=s: e.reciprocal(out=st4[s][0][:, 3:4], in_=st4[s][0][:, 2:3]),
                 reads=[st4[s][1]], writes=[st4[s][1]])
            p.op("act", lambda e, s=s, zo=zo: e.activation(out=ot[s][0][:, :], in_=ps.f[zo][:, 0:128], func=AF.Copy, scale=st4[s][0][:, 3:4]),
                 reads=[ps.fr[zo], st4[s][1]], writes=[ot[s][1]])
            p.dma("sp", f"ot{s}", o[q0:q0 + 128, h * VH:(h + 1) * VH], ot[s][0][:, :], reads=[ot[s][1]])
    p.barrier()


def build_fused(cfg, debug=False):
    D, SEQ, CTX, H, QL, KVL, FF = cfg["D"], cfg["SEQ"], cfg["CTX"], cfg["HEADS"], cfg["QL"], cfg["KVL"], cfg["FFN"]
    T = SEQ + CTX
    TQ = SEQ // 4
    MTt, MTq = T // 128, TQ // 128
    NLt = SEQ // 128
    GD = D // 4
    DC = D // 128
    NWA = QL + KVL + 2 * ROPE
    NQX = H * QKH + H * ROPE
    nc = bass.Bass("TRN2", target_bir_lowering=False)
    I = lambda name, shape, dt: nc.dram_tensor(name, shape, dt, kind="ExternalInput").ap()
    W = lambda name, shape, dt=BF16: nc.dram_tensor(name, shape, dt, kind=("ExternalOutput" if debug else "Internal")).ap()
    xin = I("xin", [T, D], F32)
    cv = I("cv", [128, D], F32)
    w_ada = [I(f"w_ada{i}", [D, 6 * D], F32) for i in range(2)]
    b_ada = [I(f"b_ada{i}", [1, 6 * D], F32) for i in range(2)]
    g_mix = [I(f"g_mix{i}", [1, D], F32) for i in range(2)]
    g_ffn = [I(f"g_ffn{i}", [1, D], F32) for i in range(2)]
    g_fin = I("g_fin", [1, D], F32)
    g_q = I("g_q", [1, QL], F32)
    g_kv = I("g_kv", [1, KVL], F32)
    zrow = I("zrow", [1, D], F32)
    w_out = I("w_out", [D, D], F32)
    w_a = I("w_a", [D, NWA], F32)
    w_uq = I("w_uq", [QL, NQX], F32)
    w_ukv = I("w_ukv", [KVL, H * 256], F32)
    w_o = I("w_o", [H * VH, D], F32)
    w_gate = [I(f"w_gate{i}", [D, FF], F32) for i in range(2)]
    w_up = [I(f"w_up{i}", [D, FF], F32) for i in range(2)]
    w_down = [I(f"w_down{i}", [FF, D], F32) for i in range(2)]
    apos = I("apos", [2 * T // 128, 128, MTt, 128], BF16)
    bch = I("bch", [2 * GD, GD], BF16)
    ident_in = I("ident", [128, 128], BF16)
    cosq = I("cosq", [TQ, H * ROPE], F32)
    sinq = I("sinq", [TQ, H * ROPE], F32)
    cosk = I("cosk", [T, ROPE], F32)
    sink = I("sink", [T, ROPE], F32)
    out = nc.dram_tensor("out", [TQ, D], F32, kind="ExternalOutput").ap()
    s_act = W("s_act", [128, D]); s_at = W("s_at", [1, 128, DC, 128])
    mod = [W(f"mod{i}", [128, 6 * D], F32) for i in range(2)]
    Hh = W("Hh", [T, D])
    Pp = W("Pp", [2 * T, D])
    a2_at = W("a2_at", [MTt, 128, 2 * GD // 128, 128])
    Fa = W("Fa", [T, D]); F_at = W("F_at", [MTt, 128, DC, 128])
    X1 = W("X1", [T, D], F32); X2 = W("X2", [T, D], F32)
    H_at = W("H_at", [MTt, 128, DC, 128])
    aa = W("aa", [T, FF]); a_at = W("a_at", [MTt, 128, FF // 128, 128])
    ca = W("ca", [T, NWA], F32)
    cqn = W("cqn", [TQ, QL]); cqn_at = W("cqn_at", [MTq, 128, QL // 128, 128])
    ckvn = W("ckvn", [T, KVL]); ckvn_at = W("ckvn_at", [MTt, 128, KVL // 128, 128])
    qx = W("qx", [TQ, NQX]); kvx = W("kvx", [T, H * 256])
    qrot = W("qrot", [TQ, H * ROPE]); krot = W("krot", [T, ROPE])
    oo = W("oo", [TQ, H * VH]); o_at = W("o_at", [MTq, 128, H * VH // 128, 128])
    X3 = W("X3", [TQ, D], F32); X4 = W("X4", [TQ, D], F32)

    with ExitStack() as st:
        p = Prog(nc, st)
        ps = PS(nc, st)
        idt = st.enter_context(nc.sbuf_tensor("ident_sb", [128, 128], BF16))
        ident = (idt, Res("ident"))
        p.dma("sp", "ident", idt[:, :], ident_in[:, :], writes=[ident[1]])
        grp = lambda mt: 0 if mt < NLt else 1

        def modrow(i, j):
            return lambda mt: mod[i][grp(mt):grp(mt) + 1, j * D:(j + 1) * D]

        def rows(ap, c0=0, c1=None):
            return lambda mt: [(0, ap[mt * 128:(mt + 1) * 128, c0:(c1 if c1 is not None else ap.shape[1])])]

        st_silu(p, cv[:, :], s_act[:, :], D)
        st_trans(p, ps, ident, rows(s_act), 1, D, s_at)
        for i in range(2):
            st_gemm(p, ps, s_at, 1, DC, [w_ada[i]], 6 * D, "bias", mod[i], brow=b_ada[i])

        def norm_mod(Xs, nt, i, jsh, jsc, g_row, out_ap):
            st_norm(p, lambda t: Xs[t * 128:(t + 1) * 128, :], nt, D, g_row,
                    lambda t: modrow(i, jsc)(t), lambda t: modrow(i, jsh)(t),
                    lambda t: out_ap[t * 128:(t + 1) * 128, :], BF16)

        def ffn(Xs, Xd, nt, i):
            norm_mod(Xs, nt, i, 3, 4, g_ffn[i][0:1, :], Hh)
            st_trans(p, ps, ident, rows(Hh), nt, D, H_at)
            st_gemm(p, ps, H_at, nt, DC, [w_gate[i], w_up[i]], FF, "silu", aa)
            st_trans(p, ps, ident, rows(aa), nt, FF, a_at)
            st_gemm(p, ps, a_at, nt, FF // 128, [w_down[i]], D, "resid", Xd, r_ap=Xs, grow=modrow(i, 5))

        norm_mod(xin, MTt, 0, 0, 1, g_mix[0][0:1, :], Hh)
        st_gemm(p, ps, apos, 2 * MTt, MTt, [Hh], D, "plain", Pp)
        for g in range(4):
            def a2src(mt, g=g):
                if mt < NLt:
                    r0, r1 = mt * 128, SEQ + mt * 128
                else:
                    r0, r1 = 2 * SEQ + (mt - NLt) * 128, 2 * SEQ + CTX + (mt - NLt) * 128
                return [(0, Pp[r0:r0 + 128, g * GD:(g + 1) * GD]), (GD, Pp[r1:r1 + 128, g * GD:(g + 1) * GD])]
            st_trans(p, ps, ident, a2src, MTt, 2 * GD, a2_at)
            st_gemm(p, ps, a2_at, MTt, 2 * GD // 128, [bch], GD, "plain", Fa[:, g * GD:(g + 1) * GD])
        st_trans(p, ps, ident, rows(Fa), MTt, D, F_at)
        st_gemm(p, ps, F_at, MTt, DC, [w_out], D, "resid", X1, r_ap=xin, grow=modrow(0, 2))
        ffn(X1, X2, MTt, 0)

        norm_mod(X2, MTt, 1, 0, 1, g_mix[1][0:1, :], Hh)
        st_trans(p, ps, ident, rows(Hh), MTt, D, H_at)
        st_gemm(p, ps, H_at, MTt, DC, [w_a], NWA, "plain", ca)
        zr = lambda n: (lambda t: zrow[0:1, 0:n])
        st_norm(p, lambda t: ca[t * 128:(t + 1) * 128, 0:QL], MTq, QL, g_q[0:1, :], zr(QL), zr(QL),
                lambda t: cqn[t * 128:(t + 1) * 128, :], BF16)
        st_norm(p, lambda t: ca[t * 128:(t + 1) * 128, QL:QL + KVL], MTt, KVL, g_kv[0:1, :], zr(KVL), zr(KVL),
                lambda t: ckvn[t * 128:(t + 1) * 128, :], BF16)
        st_trans(p, ps, ident, rows(cqn), MTq, QL, cqn_at)
        st_trans(p, ps, ident, rows(ckvn), MTt, KVL, ckvn_at)
        st_gemm(p, ps, cqn_at, MTq, QL // 128, [w_uq], NQX, "plain", qx)
        st_gemm(p, ps, ckvn_at, MTt, KVL // 128, [w_ukv], H * 256, "plain", kvx)
        rs = lambda t: slice(t * 128, (t + 1) * 128)
        h3 = lambda ap: ap.rearrange("t (h d) -> t h d", d=ROPE)
        st_muladd(p, MTq, [128, H, ROPE], BF16,
                  lambda t: qx[rs(t), 0:H * QKH].rearrange("t (h d) -> t h d", d=QKH)[:, :, NOPE:QKH],
                  lambda t: h3(cosq[rs(t), :]),
                  lambda t: h3(qx[rs(t), H * QKH:NQX]),
                  lambda t: h3(sinq[rs(t), :]),
                  lambda t: h3(qrot[rs(t), :]))
        st_muladd(p, MTt, [128, ROPE], F32,
                  lambda t: ca[rs(t), QL + KVL:QL + KVL + ROPE], lambda t: cosk[rs(t), :],
                  lambda t: ca[rs(t), QL + KVL + ROPE:NWA], lambda t: sink[rs(t), :],
                  lambda t: krot[rs(t), :])
        st_attn(p, ps, ident, H, TQ, T, qx, qrot, kvx, krot, oo)
        st_trans(p, ps, ident, rows(oo), MTq, H * VH, o_at)
        st_gemm(p, ps, o_at, MTq, H * VH // 128, [w_o], D, "resid", X3, r_ap=X2, grow=modrow(1, 2))
        ffn(X3, X4, MTq, 1)
        st_norm(p, lambda t: X4[t * 128:(t + 1) * 128, :], MTq, D, g_fin[0:1, :], zr(D), zr(D),
                lambda t: out[t * 128:(t + 1) * 128, :], F32, final=True)
        p.emit()
    return nc


def tile_at(A):
    M, K = A.shape
    return np.ascontiguousarray(A.reshape(M // 128, 128, K // 128, 128).transpose(0, 3, 2, 1))


def _swap_idx():
    q = ROPE // 4
    return np.concatenate([np.arange(q, 2 * q), np.arange(0, q), np.arange(3 * q, 4 * q), np.arange(2 * q, 3 * q)])


def host_inputs(cfg, x, c, ctx, c_ctx, w_ada, b_ada, g_mix, g_ffn, fourier_w_out, mla_w_a, mla_g_q, mla_g_kv,
                mla_w_uq, mla_w_ukv, mla_w_o, w_gate, w_up, w_down, g_final):
    f32 = np.float32
    D, SEQ, CTX, H, QL, KVL, FF, GW = cfg["D"], cfg["SEQ"], cfg["CTX"], cfg["HEADS"], cfg["QL"], cfg["KVL"], cfg["FFN"], cfg["GRID_W"]
    T, TQ, GD = SEQ + CTX, SEQ // 4, D // 4
    A = lambda z: np.ascontiguousarray(np.asarray(z, f32))
    row = lambda z: A(z).reshape(1, -1)
    sw = _swap_idx()
    wa = A(mla_w_a[0])
    w_a_ext = np.ascontiguousarray(np.concatenate([wa, wa[:, QL + KVL + sw]], 1))
    wq = A(mla_w_uq[0])
    wq3 = wq.reshape(QL, H, QKH)
    w_uq_ext = np.ascontiguousarray(np.concatenate([wq, wq3[:, :, NOPE + sw].reshape(QL, H * ROPE)], 1))
    shared = {
        "g_fin": row(g_final), "g_q": row(mla_g_q[0]), "g_kv": row(mla_g_kv[0]), "zrow": np.zeros((1, D), f32),
        "w_out": A(fourier_w_out[0]), "w_a": w_a_ext, "w_uq": w_uq_ext, "w_ukv": A(mla_w_ukv[0]), "w_o": A(mla_w_o[0]),
        "ident": np.eye(128, dtype=f32).astype(bf16),
    }
    for i in range(2):
        shared[f"w_ada{i}"] = A(w_ada[i]); shared[f"b_ada{i}"] = row(b_ada[i])
        shared[f"g_mix{i}"] = row(g_mix[i]); shared[f"g_ffn{i}"] = row(g_ffn[i])
        shared[f"w_gate{i}"] = A(w_gate[i]); shared[f"w_up{i}"] = A(w_up[i]); shared[f"w_down{i}"] = A(w_down[i])
    k = np.arange(SEQ, dtype=np.int64)
    ang = 2.0 * np.pi * ((k[:, None] * k[None, :]) % SEQ) / SEQ
    Cn = np.cos(ang) / np.sqrt(SEQ); Sn = np.sin(ang) / np.sqrt(SEQ)
    kc = np.arange(CTX, dtype=np.int64)
    angc = 2.0 * np.pi * ((kc[:, None] * kc[None, :]) % CTX) / CTX
    Cc = np.cos(angc) / np.sqrt(CTX); Sc = np.sin(angc) / np.sqrt(CTX)
    j = np.arange(GD, dtype=np.int64)
    angm = 2.0 * np.pi * ((j[:, None] * j[None, :]) % GD) / GD
    shared["bch"] = (np.concatenate([np.cos(angm), -np.sin(angm)], 0) / np.sqrt(GD)).astype(f32).astype(bf16)
    rows_n = SEQ // GW
    rr = np.repeat(np.arange(rows_n, dtype=f32), GW); cc = np.tile(np.arange(GW, dtype=f32), rows_n)
    nf = ROPE // 4
    inv = (10000.0 ** (-np.arange(nf, dtype=f32) / nf)).astype(f32)
    angr = np.concatenate([rr[:, None] * inv, cc[:, None] * inv], -1).astype(f32)
    cs, sn = np.cos(angr).astype(f32), np.sin(angr).astype(f32)
    COS = np.concatenate([cs[:, :nf], cs[:, :nf], cs[:, nf:], cs[:, nf:]], -1)
    SIN = np.concatenate([-sn[:, :nf], sn[:, :nf], -sn[:, nf:], sn[:, nf:]], -1)
    x = np.asarray(x, f32); ctx = np.asarray(ctx, f32)
    ins = []
    apos_cache = {}
    for core in range(NCORES):
        b, q = core // 4, core % 4
        perm = np.concatenate([np.arange(q * TQ, (q + 1) * TQ)] + [np.arange(r * TQ, (r + 1) * TQ) for r in range(4) if r != q])
        d = dict(shared)
        d["xin"] = np.ascontiguousarray(np.concatenate([x[b][perm], ctx[b]], 0))
        cvv = np.zeros((128, D), f32); cvv[0] = np.asarray(c, f32)[b]; cvv[1] = np.asarray(c_ctx, f32)
        d["cv"] = cvv
        if q not in apos_cache:
            Ap = np.zeros((2 * T, T), f32)
            Ap[0:SEQ, 0:SEQ] = Cn[np.ix_(perm, perm)]
            Ap[SEQ:2 * SEQ, 0:SEQ] = Sn[np.ix_(perm, perm)]
            Ap[2 * SEQ:2 * SEQ + CTX, SEQ:] = Cc
            Ap[2 * SEQ + CTX:, SEQ:] = Sc
            apos_cache[q] = tile_at(Ap.astype(bf16))
        d["apos"] = apos_cache[q]
        d["cosq"] = np.ascontiguousarray(np.tile(COS[perm[:TQ]], (1, H)))
        d["sinq"] = np.ascontiguousarray(np.tile(SIN[perm[:TQ]], (1, H)))
        d["cosk"] = np.ascontiguousarray(np.concatenate([COS[perm], np.ones((CTX, ROPE), f32)], 0))
        d["sink"] = np.ascontiguousarray(np.concatenate([SIN[perm], np.zeros((CTX, ROPE), f32)], 0))
        ins.append(d)
    return ins


_prog_cache = {}


def run_fused(cfg, **inputs):
    key = tuple(sorted(cfg.items()))
    if key not in _prog_cache:
        _prog_cache[key] = build_fused(cfg)
    nc = _prog_cache[key]
    ins = host_inputs(cfg, **inputs)
    res = run_bass_kernel_spmd(nc, ins, core_ids=list(range(NCORES)))
    SEQ, D = cfg["SEQ"], cfg["D"]
    TQ = SEQ // 4
    out = np.zeros((cfg["BATCH"], SEQ, D), np.float32)
    for core in range(NCORES):
        b, q = core // 4, core % 4
        out[b, q * TQ:(q + 1) * TQ] = res.results[core]["out"]
    return out


def kernel(x, c, ctx, c_ctx, w_ada, b_ada, g_mix, g_ffn, fourier_w_out, mla_w_a, mla_g_q, mla_g_kv,
           mla_w_uq, mla_w_ukv, mla_w_o, w_gate, w_up, w_down, g_final):
    return run_fused(CFG_FULL, x=x, c=c, ctx=ctx, c_ctx=c_ctx, w_ada=w_ada, b_ada=b_ada, g_mix=g_mix, g_ffn=g_ffn,
                     fourier_w_out=fourier_w_out, mla_w_a=mla_w_a, mla_g_q=mla_g_q, mla_g_kv=mla_g_kv,
                     mla_w_uq=mla_w_uq, mla_w_ukv=mla_w_ukv, mla_w_o=mla_w_o, w_gate=w_gate, w_up=w_up,
                     w_down=w_down, g_final=g_final)
```

```python
import numpy as np
import ml_dtypes
from contextlib import ExitStack
import concourse.bass as bass
import concourse.mybir as mybir
from concourse.bass_utils import run_bass_kernel_spmd

F32 = mybir.dt.float32
BF16 = mybir.dt.bfloat16
AF = mybir.ActivationFunctionType
ALU = mybir.AluOpType
AX = mybir.AxisListType
bf16 = ml_dtypes.bfloat16
NCORES = 8
EPS = 1e-6
NOPE = 128
ROPE = 64
VH = 128
QKH = NOPE + ROPE
SCALE = QKH ** -0.5

CFG_FULL = dict(D=4096, SEQ=4096, CTX=256, HEADS=64, QL=1536, KVL=512, FFN=11008, GRID_W=64, BATCH=2)


class Res:
    def __init__(self, name):
        self.name = name
        self.last_w = None
        self.readers = []


class Op:
    __slots__ = ("eng", "fn", "deps", "dma", "sem", "val", "signal", "key", "grp")

    def __init__(self, eng, fn, dma, key, grp):
        self.eng = eng
        self.fn = fn
        self.deps = []
        self.dma = dma
        self.sem = None
        self.val = 0
        self.signal = dma
        self.key = key
        self.grp = grp


ENGS = ("pe", "act", "dve", "pool", "sp")
CENGS = ("pe", "act", "dve", "pool")
NGRP = 4
STQ = "pool"
ARENA = 102400


class Prog:
    def __init__(self, nc, stack):
        self.nc = nc
        self.stack = stack
        self.ops = {e: [] for e in ENGS}
        self.eng_sem = {(e, g): stack.enter_context(nc.semaphore(f"cs_{e}{g}")) for e in CENGS for g in range(NGRP)}
        self.dma_sems = {}
        self.dma_cnt = {}
        self.out_dmas = []
        self.stage = 0
        self.pending = {e: None for e in ENGS}
        self.stage_dmas = []
        self.stage_keys = {}
        self.arena = stack.enter_context(nc.sbuf_tensor("arena", [128, ARENA], BF16))
        self.off = 0

    def alloc(self, name, shape, dt):
        per = 1
        for s in shape[1:]:
            per *= s
        nb = per * (4 if dt == F32 else 2)
        nb = (nb + 63) // 64 * 64
        ne = nb // 2
        assert self.off + ne <= ARENA, (name, shape, self.off)
        v = self.arena[0:shape[0], self.off:self.off + per * (2 if dt == F32 else 1)]
        self.off += ne
        if dt == F32:
            v = v.bitcast(F32)
        if len(shape) == 3:
            v = v.rearrange("p (a b) -> p a b", b=shape[2])
        return v, Res(name)

    def barrier(self):
        deps = []
        for e in CENGS:
            for op in reversed(self.ops[e]):
                if not op.dma:
                    deps.append(op)
                    break
        deps.extend(self.stage_dmas)
        self.stage_dmas = []
        for e in ENGS:
            prev = self.pending[e] or []
            self.pending[e] = prev + deps
        self.stage += 1
        self.off = 0
        self.stage_keys = {}

    def _add(self, eng, fn, reads, writes, dma=False, key=None, chain=False):
        grp = self.stage % NGRP
        if key is not None:
            if key not in self.stage_keys:
                self.stage_keys[key] = len(self.stage_keys)
            key = f"{grp}_{self.stage_keys[key]}"
        op = Op(eng, fn, dma, key, grp)
        deps = []
        for r in reads:
            if r.last_w is not None:
                deps.append(r.last_w)
        for w in writes:
            if w.last_w is not None:
                if not (chain and w.last_w.dma and w.last_w.key == key):
                    deps.append(w.last_w)
                else:
                    deps.extend(w.last_w.deps)
            deps.extend(w.readers)
        if self.pending[eng]:
            deps.extend(self.pending[eng])
            self.pending[eng] = None
        seen = set()
        for d in deps:
            if id(d) not in seen and d is not op:
                seen.add(id(d))
                op.deps.append(d)
        for r in reads:
            r.readers.append(op)
        for w in writes:
            w.last_w = op
            w.readers = []
        if dma:
            if key not in self.dma_sems:
                self.dma_sems[key] = self.stack.enter_context(self.nc.semaphore("ds_" + key))
                self.dma_cnt[key] = 0
            self.dma_cnt[key] += 16
            op.sem = self.dma_sems[key]
            op.val = self.dma_cnt[key]
            self.stage_dmas.append(op)
        self.ops[eng].append(op)
        return op

    def op(self, eng, fn, reads=(), writes=()):
        return self._add(eng, fn, reads, writes)

    def dma(self, eng, key, out, in_, reads=(), writes=(), chain=False, is_out=False):
        op = self._add(eng, lambda e, out=out, in_=in_: e.dma_start(out=out, in_=in_), reads, writes,
                       dma=True, key=key, chain=chain)
        if is_out:
            self.out_dmas.append(op)
        return op

    def emit(self):
        fin = Op("sp", None, False, None, 0)
        fin.deps = list(self.out_dmas)
        self.ops["sp"].append(fin)
        for e in ENGS:
            for op in self.ops[e]:
                for d in op.deps:
                    if not d.dma:
                        if d.eng == "pe" and op.eng == "pe" and not op.dma:
                            continue
                        d.signal = True
        for e in CENGS:
            cnt = [0] * NGRP
            for op in self.ops[e]:
                if not op.dma and op.signal:
                    cnt[op.grp] += 1
                    op.sem = self.eng_sem[(e, op.grp)]
                    op.val = cnt[op.grp]
        nc = self.nc
        ops = self.ops

        def run(e, eng):
            seen = {}
            for op in ops[e]:
                need = {}
                for d in op.deps:
                    if not d.dma and d.eng == "pe" and e == "pe" and not op.dma:
                        continue
                    k = id(d.sem)
                    if seen.get(k, 0) >= d.val:
                        continue
                    if k not in need or need[k][1] < d.val:
                        need[k] = (d.sem, d.val)
                for k, (s, v) in need.items():
                    eng.wait_ge(s, v)
                    seen[k] = v
                if op.fn is None:
                    continue
                ins = op.fn(eng)
                if op.signal:
                    ins.then_inc(op.sem, 16 if op.dma else 1)

        with nc.Block() as block:
            @block.tensor
            def _(eng):
                run("pe", eng)

            @block.scalar
            def _(eng):
                run("act", eng)

            @block.vector
            def _(eng):
                run("dve", eng)

            @block.gpsimd
            def _(eng):
                run("pool", eng)

            @block.sync
            def _(eng):
                run("sp", eng)


def _nblocks(n, nb=512):
    return [(i, min(nb, n - i)) for i in range(0, n, nb)]


class PS:
    def __init__(self, nc, st):
        self.f = [st.enter_context(nc.psum_tensor(f"psf{i}", [128, 512], F32)) for i in range(5)]
        self.fr = [Res(f"psf{i}") for i in range(5)]
        self.t = [st.enter_context(nc.psum_tensor(f"pst{i}", [128, 4, 128], BF16)) for i in range(3)]
        self.tr = [Res(f"pst{i}") for i in range(3)]


def st_gemm(p, ps, at, MT, KC, bs, N, epi, out, r_ap=None, grow=None, brow=None):
    odt = out.dtype
    nB = len(bs)
    NB = 512
    bv = [b.rearrange("(kc p) n -> p kc n", p=128) for b in bs]
    RES_AT = MT * KC * 256 <= 64 * 1024
    NAT = MT if RES_AT else 3
    at_bytes = NAT * KC * 256
    nbuf = 2 if (KC * NB * 2 * nB * 2 + at_bytes + 24 * 1024 <= ARENA * 2) else 1
    bsb = [[p.alloc(f"bsb{i}_{j}", [128, KC, NB], BF16) for i in range(nB)] for j in range(nbuf)]
    att = [p.alloc(f"at{i}", [128, KC, 128], BF16) for i in range(NAT)]
    if RES_AT:
        for mt in range(MT):
            p.dma("sp", "atres", att[mt][0][:, :, :], at[mt], writes=[att[mt][1]])
    ot = [p.alloc(f"ot{i}", [128, NB], odt) for i in range(2)]
    if epi != "plain":
        tmp = [p.alloc(f"tmp{i}", [128, NB], F32) for i in range(2)]
    if epi == "resid":
        rt = [p.alloc(f"rt{i}", [128, NB], F32) for i in range(2)]
        gt = [p.alloc(f"gt{i}", [128, NB], F32) for i in range(2)]
    if epi == "bias":
        bt = [p.alloc(f"bt{i}", [128, NB], F32) for i in range(2)]
    NPS = 4 // nB
    it = 0
    KG = 8
    for bi, (n0, nb) in enumerate(_nblocks(N, NB)):
        bb = bsb[bi % nbuf]
        for i in range(nB):
            for k0 in range(0, KC, KG):
                k1 = min(KC, k0 + KG)
                p.dma("pool", f"b{i}_{bi % nbuf}", bb[i][0][:, k0:k1, 0:nb], bv[i][:, k0:k1, n0:n0 + nb],
                      writes=[bb[i][1]], chain=(k0 > 0))
        if epi == "bias":
            p.dma("pool", f"bt{bi % 2}", bt[bi % 2][0][:, 0:nb], brow[0:1, n0:n0 + nb].partition_broadcast(128), writes=[bt[bi % 2][1]])
        for mt in range(MT):
            s = it % 2
            a = it % NAT
            q = it % NPS
            it += 1
            if RES_AT:
                a = mt
            else:
                p.dma("sp", f"at{a}", att[a][0][:, :, :], at[mt], writes=[att[a][1]])
            if epi == "resid":
                p.dma("act", f"rt{s}", rt[s][0][:, 0:nb], r_ap[mt * 128:(mt + 1) * 128, n0:n0 + nb], writes=[rt[s][1]])
                p.dma("act", f"gt{s}", gt[s][0][:, 0:nb], grow(mt)[0:1, n0:n0 + nb].partition_broadcast(128), writes=[gt[s][1]])
            for i in range(nB):
                z = q * nB + i

                def mm(e, i=i, a=a, z=z, nb=nb, bb=bb):
                    ins = None
                    for kc in range(KC):
                        ins = e.matmul(ps.f[z][:, 0:nb], att[a][0][:, kc, :], bb[i][0][:, kc, 0:nb],
                                       start=(kc == 0), stop=(kc == KC - 1))
                    return ins
                p.op("pe", mm, reads=[att[a][1], bb[i][1]], writes=[ps.fr[z]])
            z0 = q * nB
            if epi == "plain":
                if it % 2 == 0:
                    p.op("act", lambda e, s=s, z0=z0, nb=nb: e.activation(out=ot[s][0][:, 0:nb], in_=ps.f[z0][:, 0:nb], func=AF.Copy),
                         reads=[ps.fr[z0]], writes=[ot[s][1]])
                else:
                    p.op("dve", lambda e, s=s, z0=z0, nb=nb: e.tensor_copy(out=ot[s][0][:, 0:nb], in_=ps.f[z0][:, 0:nb]),
                         reads=[ps.fr[z0]], writes=[ot[s][1]])
            elif epi == "silu":
                p.op("act", lambda e, s=s, z0=z0, nb=nb: e.activation(out=tmp[s][0][:, 0:nb], in_=ps.f[z0][:, 0:nb], func=AF.Silu),
                     reads=[ps.fr[z0]], writes=[tmp[s][1]])
                p.op("dve", lambda e, s=s, z0=z0, nb=nb: e.tensor_tensor(out=ot[s][0][:, 0:nb], in0=tmp[s][0][:, 0:nb], in1=ps.f[z0 + 1][:, 0:nb], op=ALU.mult),
                     reads=[tmp[s][1], ps.fr[z0 + 1]], writes=[ot[s][1]])
            elif epi == "resid":
                p.op("dve", lambda e, s=s, z0=z0, nb=nb: e.tensor_tensor(out=tmp[s][0][:, 0:nb], in0=ps.f[z0][:, 0:nb], in1=gt[s][0][:, 0:nb], op=ALU.mult),
                     reads=[ps.fr[z0], gt[s][1]], writes=[tmp[s][1]])
                p.op("pool", lambda e, s=s, nb=nb: e.tensor_tensor(out=ot[s][0][:, 0:nb], in0=tmp[s][0][:, 0:nb], in1=rt[s][0][:, 0:nb], op=ALU.add),
                     reads=[tmp[s][1], rt[s][1]], writes=[ot[s][1]])
            else:
                p.op("dve", lambda e, s=s, z0=z0, nb=nb, bi=bi: e.tensor_tensor(out=ot[s][0][:, 0:nb], in0=ps.f[z0][:, 0:nb], in1=bt[bi % 2][0][:, 0:nb], op=ALU.add),
                     reads=[ps.fr[z0], bt[bi % 2][1]], writes=[ot[s][1]])
            p.dma("sp" if RES_AT else STQ, f"ot{s}", out[mt * 128:(mt + 1) * 128, n0:n0 + nb], ot[s][0][:, 0:nb], reads=[ot[s][1]])
    p.barrier()


def st_trans(p, ps, ident, src_fn, MT, K, at_out):
    KC = K // 128
    xt = [p.alloc(f"xt{i}", [128, K], BF16) for i in range(2)]
    att = [p.alloc(f"att{i}", [128, KC, 128], BF16) for i in range(2)]
    iT = 0
    for mt in range(MT):
        s = mt % 2
        for j, (c0, ap) in enumerate(src_fn(mt)):
            w = ap.shape[1]
            p.dma("sp" if j % 2 == 0 else "act", f"xt{s}", xt[s][0][:, c0:c0 + w], ap, writes=[xt[s][1]], chain=(j > 0))
        for c0 in range(0, KC, 4):
            c1 = min(KC, c0 + 4)
            z = iT % 3
            iT += 1

            def tr(e, s=s, z=z, c0=c0, c1=c1):
                ins = None
                for c in range(c0, c1):
                    ins = e.transpose(out=ps.t[z][:, c - c0, :], in_=xt[s][0][:, c * 128:(c + 1) * 128], identity=ident[0][:, :])
                return ins
            p.op("pe", tr, reads=[xt[s][1], ident[1]], writes=[ps.tr[z]])
            if (c0 // 4) % 2 == 0:
                p.op("dve", lambda e, s=s, z=z, c0=c0, c1=c1: e.tensor_copy(out=att[s][0][:, c0:c1, :], in_=ps.t[z][:, 0:c1 - c0, :]),
                     reads=[ps.tr[z]], writes=[att[s][1]])
            else:
                p.op("act", lambda e, s=s, z=z, c0=c0, c1=c1: e.activation(out=att[s][0][:, c0:c1, :], in_=ps.t[z][:, 0:c1 - c0, :], func=AF.Copy),
                     reads=[ps.tr[z]], writes=[att[s][1]])
        p.dma("pool", f"att{s}", at_out[mt], att[s][0][:, :, :], reads=[att[s][1]])
    p.barrier()


def st_norm(p, x_fn, NT, D, g_row, sc_fn, sh_fn, out_fn, out_dt, final=False):
    gb = p.alloc("gb", [128, D], F32)
    xt = [p.alloc(f"xt{i}", [128, D], F32) for i in range(2)]
    sct = [p.alloc(f"sct{i}", [128, D], F32) for i in range(2)]
    sht = [p.alloc(f"sht{i}", [128, D], F32) for i in range(2)]
    sq = p.alloc("sq", [128, D], F32)
    ss = [p.alloc(f"ss{i}", [128, 4], F32) for i in range(2)]
    ot = [p.alloc(f"ot{i}", [128, D], out_dt) for i in range(2)]
    p.dma("pool", "gb", gb[0][:, :], g_row.partition_broadcast(128), writes=[gb[1]])
    for t in range(NT):
        s = t % 2
        p.dma("sp", f"xt{s}", xt[s][0][:, :], x_fn(t), writes=[xt[s][1]])
        p.dma("pool", f"sct{s}", sct[s][0][:, :], sc_fn(t).partition_broadcast(128), writes=[sct[s][1]])
        p.dma("pool", f"sht{s}", sht[s][0][:, :], sh_fn(t).partition_broadcast(128), writes=[sht[s][1]])
        p.op("act", lambda e, s=s: e.activation(out=sq[0][:, :], in_=xt[s][0][:, :], func=AF.Square, accum_out=ss[s][0][:, 0:1]),
             reads=[xt[s][1]], writes=[sq[1], ss[s][1]])
        p.op("dve", lambda e, s=s: e.tensor_scalar(out=ss[s][0][:, 1:2], in0=ss[s][0][:, 0:1], scalar1=1.0 / D, scalar2=EPS, op0=ALU.mult, op1=ALU.add),
             reads=[ss[s][1]], writes=[ss[s][1]])
        p.op("act", lambda e, s=s: e.activation(out=ss[s][0][:, 2:3], in_=ss[s][0][:, 1:2], func=AF.Sqrt),
             reads=[ss[s][1]], writes=[ss[s][1]])
        p.op("dve", lambda e, s=s: e.reciprocal(out=ss[s][0][:, 3:4], in_=ss[s][0][:, 2:3]),
             reads=[ss[s][1]], writes=[ss[s][1]])
        p.op("dve", lambda e, s=s: e.scalar_tensor_tensor(out=sct[s][0][:, :], in0=sct[s][0][:, :], scalar=1.0, in1=gb[0][:, :], op0=ALU.add, op1=ALU.mult),
             reads=[sct[s][1], gb[1]], writes=[sct[s][1]])
        p.op("dve", lambda e, s=s: e.scalar_tensor_tensor(out=xt[s][0][:, :], in0=xt[s][0][:, :], scalar=ss[s][0][:, 3:4], in1=sct[s][0][:, :], op0=ALU.mult, op1=ALU.mult),
             reads=[xt[s][1], ss[s][1], sct[s][1]], writes=[xt[s][1]])
        p.op("pool", lambda e, s=s: e.tensor_tensor(out=ot[s][0][:, :], in0=xt[s][0][:, :], in1=sht[s][0][:, :], op=ALU.add),
             reads=[xt[s][1], sht[s][1]], writes=[ot[s][1]])
        p.dma("pool", f"ot{s}", out_fn(t), ot[s][0][:, :], reads=[ot[s][1]], is_out=final)
    p.barrier()


def st_silu(p, a_ap, out_ap, Fd):
    at = p.alloc("sa", [128, Fd], F32)
    ot = p.alloc("so", [128, Fd], BF16)
    p.dma("sp", "sa", at[0][:, :], a_ap, writes=[at[1]])
    p.op("act", lambda e: e.activation(out=ot[0][:, :], in_=at[0][:, :], func=AF.Silu), reads=[at[1]], writes=[ot[1]])
    p.dma("sp", "so", out_ap, ot[0][:, :], reads=[ot[1]])
    p.barrier()


def st_muladd(p, NT, shape, a_dt, a_fn, b_fn, c_fn, d_fn, out_fn):
    dts = [a_dt, F32, a_dt, F32]
    fns = [a_fn, b_fn, c_fn, d_fn]
    names = "abcd"
    tl = [[p.alloc(f"{names[j]}{i}", shape, dts[j]) for i in range(2)] for j in range(4)]
    t1 = [p.alloc(f"t1{i}", shape, F32) for i in range(2)]
    t2 = [p.alloc(f"t2{i}", shape, F32) for i in range(2)]
    ot = [p.alloc(f"ot{i}", shape, BF16) for i in range(2)]
    for t in range(NT):
        s = t % 2
        for j in range(4):
            p.dma("sp" if j % 2 == 0 else "pool", f"{names[j]}{s}", tl[j][s][0], fns[j](t), writes=[tl[j][s][1]])
        p.op("dve", lambda e, s=s: e.tensor_tensor(out=t1[s][0], in0=tl[0][s][0], in1=tl[1][s][0], op=ALU.mult),
             reads=[tl[0][s][1], tl[1][s][1]], writes=[t1[s][1]])
        p.op("pool", lambda e, s=s: e.tensor_tensor(out=t2[s][0], in0=tl[2][s][0], in1=tl[3][s][0], op=ALU.mult),
             reads=[tl[2][s][1], tl[3][s][1]], writes=[t2[s][1]])
        p.op("dve", lambda e, s=s: e.tensor_tensor(out=ot[s][0], in0=t1[s][0], in1=t2[s][0], op=ALU.add),
             reads=[t1[s][1], t2[s][1]], writes=[ot[s][1]])
        p.dma("act", f"ot{s}", out_fn(t), ot[s][0], reads=[ot[s][1]])
    p.barrier()


def st_attn(p, ps, ident, H, NQ, NK, qx, qrot, kvx, krot, o):
    NKC = NK // 128
    NQT = NQ // 128
    kbl = _nblocks(NK, 512)
    krtok = p.alloc("krtok", [128, NKC, ROPE], BF16)
    krT = p.alloc("krT", [64, NK], BF16)
    qtok = [p.alloc(f"qtok{i}", [128, NQT, NOPE], BF16) for i in range(2)]
    qrtok = [p.alloc(f"qrtok{i}", [128, NQT, ROPE], BF16) for i in range(2)]
    ktok = [p.alloc(f"ktok{i}", [128, NKC, NOPE], BF16) for i in range(2)]
    vt = [p.alloc(f"vt{i}", [128, NKC, VH], BF16) for i in range(2)]
    qn = [p.alloc(f"qn{i}", [128, NQ], BF16) for i in range(2)]
    qr_ = [p.alloc(f"qr{i}", [64, NQ], BF16) for i in range(2)]
    kn = [p.alloc(f"kn{i}", [128, NK], BF16) for i in range(2)]
    S = [p.alloc(f"S{i}", [128, NK], F32) for i in range(2)]
    P = [p.alloc(f"P{i}", [128, NK], BF16) for i in range(2)]
    PT = [p.alloc(f"PT{i}", [128, NKC, 128], BF16) for i in range(2)]
    st4 = [p.alloc(f"st{i}", [128, 4], F32) for i in range(2)]
    ot = [p.alloc(f"ot{i}", [128, 128], BF16) for i in range(2)]
    cnt = {"S": 0, "T": 0, "q": 0, "e": 0}

    def trans_into(src, dst, nchunk, rows_out):
        for c0 in range(0, nchunk, 4):
            c1 = min(nchunk, c0 + 4)
            z = cnt["T"] % 3
            cnt["T"] += 1

            def tr(e, z=z, c0=c0, c1=c1):
                ins = None
                for c in range(c0, c1):
                    ins = e.transpose(out=ps.t[z][0:rows_out, c - c0, :], in_=src[0][:, c, :], identity=ident[0][:, :])
                return ins
            p.op("pe", tr, reads=[src[1], ident[1]], writes=[ps.tr[z]])
            cnt["e"] += 1
            dv = dst[0][:, c0 * 128:c1 * 128].rearrange("p (a b) -> p a b", b=128)
            if cnt["e"] % 2 == 0:
                p.op("dve", lambda e, z=z, c0=c0, c1=c1, dv=dv: e.tensor_copy(out=dv, in_=ps.t[z][0:rows_out, 0:c1 - c0, :]),
                     reads=[ps.tr[z]], writes=[dst[1]])
            else:
                p.op("act", lambda e, z=z, c0=c0, c1=c1, dv=dv: e.activation(out=dv, in_=ps.t[z][0:rows_out, 0:c1 - c0, :], func=AF.Copy),
                     reads=[ps.tr[z]], writes=[dst[1]])

    p.dma("sp", "krtok", krtok[0][:, :, :], krot.rearrange("(c p) d -> p c d", p=128), writes=[krtok[1]])
    trans_into(krtok, krT, NKC, ROPE)
    it = 0
    for h in range(H):
        b = h % 2
        p.dma("sp", f"qtok{b}", qtok[b][0][:, :, :], qx[:, h * QKH:h * QKH + NOPE].rearrange("(c p) d -> p c d", p=128), writes=[qtok[b][1]])
        p.dma("sp", f"qrtok{b}", qrtok[b][0][:, :, :], qrot[:, h * ROPE:(h + 1) * ROPE].rearrange("(c p) d -> p c d", p=128), writes=[qrtok[b][1]])
        p.dma("pool", f"ktok{b}", ktok[b][0][:, :, :], kvx[:, h * 256:h * 256 + NOPE].rearrange("(c p) d -> p c d", p=128), writes=[ktok[b][1]])
        p.dma("act", f"vt{b}", vt[b][0][:, :, :], kvx[:, h * 256 + NOPE:(h + 1) * 256].rearrange("(c p) d -> p c d", p=128), writes=[vt[b][1]])
        trans_into(ktok[b], kn[b], NKC, 128)
        trans_into(qtok[b], qn[b], NQT, 128)
        trans_into(qrtok[b], qr_[b], NQT, ROPE)
        for qt in range(NQT):
            s = it % 2
            it += 1
            q0 = qt * 128
            for bi, (k0, kb) in enumerate(kbl):
                z = cnt["S"] % 3
                cnt["S"] += 1

                def mmS(e, b=b, z=z, q0=q0, k0=k0, kb=kb):
                    e.matmul(ps.f[z][:, 0:kb], qn[b][0][:, q0:q0 + 128], kn[b][0][:, k0:k0 + kb], start=True, stop=False)
                    return e.matmul(ps.f[z][:, 0:kb], qr_[b][0][:, q0:q0 + 128], krT[0][:, k0:k0 + kb], start=False, stop=True)
                p.op("pe", mmS, reads=[qn[b][1], qr_[b][1], kn[b][1], krT[1]], writes=[ps.fr[z]])
                if bi % 2 == 0:
                    p.op("dve", lambda e, s=s, z=z, k0=k0, kb=kb: e.tensor_copy(out=S[s][0][:, k0:k0 + kb], in_=ps.f[z][:, 0:kb]),
                         reads=[ps.fr[z]], writes=[S[s][1]])
                else:
                    p.op("act", lambda e, s=s, z=z, k0=k0, kb=kb: e.activation(out=S[s][0][:, k0:k0 + kb], in_=ps.f[z][:, 0:kb], func=AF.Copy),
                         reads=[ps.fr[z]], writes=[S[s][1]])
            p.op("dve", lambda e, s=s: e.reduce_max(out=st4[s][0][:, 0:1], in_=S[s][0][:, :], axis=AX.X),
                 reads=[S[s][1]], writes=[st4[s][1]])
            p.op("dve", lambda e, s=s: e.tensor_scalar(out=st4[s][0][:, 1:2], in0=st4[s][0][:, 0:1], scalar1=-SCALE, scalar2=None, op0=ALU.mult),
                 reads=[st4[s][1]], writes=[st4[s][1]])
            p.op("act", lambda e, s=s: e.activation(out=P[s][0][:, :], in_=S[s][0][:, :], func=AF.Exp, bias=st4[s][0][:, 1:2], scale=SCALE, accum_out=st4[s][0][:, 2:3]),
                 reads=[S[s][1], st4[s][1]], writes=[P[s][1], st4[s][1]])
            Pv = (P[s][0].rearrange("p (a b) -> p a b", b=128), P[s][1])
            PTv = (PT[s][0].rearrange("p a b -> p (a b)"), PT[s][1])
            trans_into(Pv, PTv, NKC, 128)
            zo = 3 + (it % 2)

            def mmO(e, s=s, b=b, zo=zo):
                ins = None
                for c in range(NKC):
                    ins = e.matmul(ps.f[zo][:, 0:128], PT[s][0][:, c, :], vt[b][0][:, c, :], start=(c == 0), stop=(c == NKC - 1))
                return ins
            p.op("pe", mmO, reads=[PT[s][1], vt[b][1]], writes=[ps.fr[zo]])
            p.op("dve", lambda e, s=s: e.reciprocal(out=st4[s][0][:, 3:4], in_=st4[s][0][:, 2:3]),
                 reads=[st4[s][1]], writes=[st4[s][1]])
            p.op("act", lambda e, s=s, zo=zo: e.activation(out=ot[s][0][:, :], in_=ps.f[zo][:, 0:128], func=AF.Copy, scale=st4[s][0][:, 3:4]),
                 reads=[ps.fr[zo], st4[s][1]], writes=[ot[s][1]])
            p.dma("pool", f"ot{s}", o[q0:q0 + 128, h * VH:(h + 1) * VH], ot[s][0][:, :], reads=[ot[s][1]])
    p.barrier()


def build_fused(cfg, debug=False):
    D, SEQ, CTX, H, QL, KVL, FF = cfg["D"], cfg["SEQ"], cfg["CTX"], cfg["HEADS"], cfg["QL"], cfg["KVL"], cfg["FFN"]
    T = SEQ + CTX
    TQ = SEQ // 4
    MTt, MTq = T // 128, TQ // 128
    NLt = SEQ // 128
    GD = D // 4
    DC = D // 128
    NWA = QL + KVL + 2 * ROPE
    NQX = H * QKH + H * ROPE
    nc = bass.Bass("TRN2", target_bir_lowering=False)
    I = lambda name, shape, dt: nc.dram_tensor(name, shape, dt, kind="ExternalInput").ap()
    W = lambda name, shape, dt=BF16: nc.dram_tensor(name, shape, dt, kind=("ExternalOutput" if debug else "Internal")).ap()
    xin = I("xin", [T, D], F32)
    cv = I("cv", [128, D], F32)
    w_ada = [I(f"w_ada{i}", [D, 6 * D], F32) for i in range(2)]
    b_ada = [I(f"b_ada{i}", [1, 6 * D], F32) for i in range(2)]
    g_mix = [I(f"g_mix{i}", [1, D], F32) for i in range(2)]
    g_ffn = [I(f"g_ffn{i}", [1, D], F32) for i in range(2)]
    g_fin = I("g_fin", [1, D], F32)
    g_q = I("g_q", [1, QL], F32)
    g_kv = I("g_kv", [1, KVL], F32)
    zrow = I("zrow", [1, D], F32)
    w_out = I("w_out", [D, D], F32)
    w_a = I("w_a", [D, NWA], F32)
    w_uq = I("w_uq", [QL, NQX], F32)
    w_ukv = I("w_ukv", [KVL, H * 256], F32)
    w_o = I("w_o", [H * VH, D], F32)
    w_gate = [I(f"w_gate{i}", [D, FF], F32) for i in range(2)]
    w_up = [I(f"w_up{i}", [D, FF], F32) for i in range(2)]
    w_down = [I(f"w_down{i}", [FF, D], F32) for i in range(2)]
    apos = I("apos", [2 * T // 128, 128, MTt, 128], BF16)
    bch = I("bch", [2 * GD, GD], BF16)
    ident_in = I("ident", [128, 128], BF16)
    cosq = I("cosq", [TQ, H * ROPE], F32)
    sinq = I("sinq", [TQ, H * ROPE], F32)
    cosk = I("cosk", [T, ROPE], F32)
    sink = I("sink", [T, ROPE], F32)
    out = nc.dram_tensor("out", [TQ, D], F32, kind="ExternalOutput").ap()
    s_act = W("s_act", [128, D]); s_at = W("s_at", [1, 128, DC, 128])
    mod = [W(f"mod{i}", [128, 6 * D], F32) for i in range(2)]
    Hh = W("Hh", [T, D])
    Pp = W("Pp", [2 * T, D])
    a2_at = W("a2_at", [MTt, 128, 2 * GD // 128, 128])
    Fa = W("Fa", [T, D]); F_at = W("F_at", [MTt, 128, DC, 128])
    X1 = W("X1", [T, D], F32); X2 = W("X2", [T, D], F32)
    H_at = W("H_at", [MTt, 128, DC, 128])
    aa = W("aa", [T, FF]); a_at = W("a_at", [MTt, 128, FF // 128, 128])
    ca = W("ca", [T, NWA], F32)
    cqn = W("cqn", [TQ, QL]); cqn_at = W("cqn_at", [MTq, 128, QL // 128, 128])
    ckvn = W("ckvn", [T, KVL]); ckvn_at = W("ckvn_at", [MTt, 128, KVL // 128, 128])
    qx = W("qx", [TQ, NQX]); kvx = W("kvx", [T, H * 256])
    qrot = W("qrot", [TQ, H * ROPE]); krot = W("krot", [T, ROPE])
    oo = W("oo", [TQ, H * VH]); o_at = W("o_at", [MTq, 128, H * VH // 128, 128])
    X3 = W("X3", [TQ, D], F32); X4 = W("X4", [TQ, D], F32)

    with ExitStack() as st:
        p = Prog(nc, st)
        ps = PS(nc, st)
        idt = st.enter_context(nc.sbuf_tensor("ident_sb", [128, 128], BF16))
        ident = (idt, Res("ident"))
        p.dma("sp", "ident", idt[:, :], ident_in[:, :], writes=[ident[1]])
        grp = lambda mt: 0 if mt < NLt else 1

        def modrow(i, j):
            return lambda mt: mod[i][grp(mt):grp(mt) + 1, j * D:(j + 1) * D]

        def rows(ap, c0=0, c1=None):
            return lambda mt: [(0, ap[mt * 128:(mt + 1) * 128, c0:(c1 if c1 is not None else ap.shape[1])])]

        st_silu(p, cv[:, :], s_act[:, :], D)
        st_trans(p, ps, ident, rows(s_act), 1, D, s_at)
        for i in range(2):
            st_gemm(p, ps, s_at, 1, DC, [w_ada[i]], 6 * D, "bias", mod[i], brow=b_ada[i])

        def norm_mod(Xs, nt, i, jsh, jsc, g_row, out_ap):
            st_norm(p, lambda t: Xs[t * 128:(t + 1) * 128, :], nt, D, g_row,
                    lambda t: modrow(i, jsc)(t), lambda t: modrow(i, jsh)(t),
                    lambda t: out_ap[t * 128:(t + 1) * 128, :], BF16)

        def ffn(Xs, Xd, nt, i):
            norm_mod(Xs, nt, i, 3, 4, g_ffn[i][0:1, :], Hh)
            st_trans(p, ps, ident, rows(Hh), nt, D, H_at)
            st_gemm(p, ps, H_at, nt, DC, [w_gate[i], w_up[i]], FF, "silu", aa)
            st_trans(p, ps, ident, rows(aa), nt, FF, a_at)
            st_gemm(p, ps, a_at, nt, FF // 128, [w_down[i]], D, "resid", Xd, r_ap=Xs, grow=modrow(i, 5))

        norm_mod(xin, MTt, 0, 0, 1, g_mix[0][0:1, :], Hh)
        st_gemm(p, ps, apos, 2 * MTt, MTt, [Hh], D, "plain", Pp)
        for g in range(4):
            def a2src(mt, g=g):
                if mt < NLt:
                    r0, r1 = mt * 128, SEQ + mt * 128
                else:
                    r0, r1 = 2 * SEQ + (mt - NLt) * 128, 2 * SEQ + CTX + (mt - NLt) * 128
                return [(0, Pp[r0:r0 + 128, g * GD:(g + 1) * GD]), (GD, Pp[r1:r1 + 128, g * GD:(g + 1) * GD])]
            st_trans(p, ps, ident, a2src, MTt, 2 * GD, a2_at)
            st_gemm(p, ps, a2_at, MTt, 2 * GD // 128, [bch], GD, "plain", Fa[:, g * GD:(g + 1) * GD])
        st_trans(p, ps, ident, rows(Fa), MTt, D, F_at)
        st_gemm(p, ps, F_at, MTt, DC, [w_out], D, "resid", X1, r_ap=xin, grow=modrow(0, 2))
        ffn(X1, X2, MTt, 0)

        norm_mod(X2, MTt, 1, 0, 1, g_mix[1][0:1, :], Hh)
        st_trans(p, ps, ident, rows(Hh), MTt, D, H_at)
        st_gemm(p, ps, H_at, MTt, DC, [w_a], NWA, "plain", ca)
        zr = lambda n: (lambda t: zrow[0:1, 0:n])
        st_norm(p, lambda t: ca[t * 128:(t + 1) * 128, 0:QL], MTq, QL, g_q[0:1, :], zr(QL), zr(QL),
                lambda t: cqn[t * 128:(t + 1) * 128, :], BF16)
        st_norm(p, lambda t: ca[t * 128:(t + 1) * 128, QL:QL + KVL], MTt, KVL, g_kv[0:1, :], zr(KVL), zr(KVL),
                lambda t: ckvn[t * 128:(t + 1) * 128, :], BF16)
        st_trans(p, ps, ident, rows(cqn), MTq, QL, cqn_at)
        st_trans(p, ps, ident, rows(ckvn), MTt, KVL, ckvn_at)
        st_gemm(p, ps, cqn_at, MTq, QL // 128, [w_uq], NQX, "plain", qx)
        st_gemm(p, ps, ckvn_at, MTt, KVL // 128, [w_ukv], H * 256, "plain", kvx)
        rs = lambda t: slice(t * 128, (t + 1) * 128)
        h3 = lambda ap: ap.rearrange("t (h d) -> t h d", d=ROPE)
        st_muladd(p, MTq, [128, H, ROPE], BF16,
                  lambda t: qx[rs(t), 0:H * QKH].rearrange("t (h d) -> t h d", d=QKH)[:, :, NOPE:QKH],
                  lambda t: h3(cosq[rs(t), :]),
                  lambda t: h3(qx[rs(t), H * QKH:NQX]),
                  lambda t: h3(sinq[rs(t), :]),
                  lambda t: h3(qrot[rs(t), :]))
        st_muladd(p, MTt, [128, ROPE], F32,
                  lambda t: ca[rs(t), QL + KVL:QL + KVL + ROPE], lambda t: cosk[rs(t), :],
                  lambda t: ca[rs(t), QL + KVL + ROPE:NWA], lambda t: sink[rs(t), :],
                  lambda t: krot[rs(t), :])
        st_attn(p, ps, ident, H, TQ, T, qx, qrot, kvx, krot, oo)
        st_trans(p, ps, ident, rows(oo), MTq, H * VH, o_at)
        st_gemm(p, ps, o_at, MTq, H * VH // 128, [w_o], D, "resid", X3, r_ap=X2, grow=modrow(1, 2))
        ffn(X3, X4, MTq, 1)
        st_norm(p, lambda t: X4[t * 128:(t + 1) * 128, :], MTq, D, g_fin[0:1, :], zr(D), zr(D),
                lambda t: out[t * 128:(t + 1) * 128, :], F32, final=True)
        p.emit()
    return nc


def tile_at(A):
    M, K = A.shape
    return np.ascontiguousarray(A.reshape(M // 128, 128, K // 128, 128).transpose(0, 3, 2, 1))


def _swap_idx():
    q = ROPE // 4
    return np.concatenate([np.arange(q, 2 * q), np.arange(0, q), np.arange(3 * q, 4 * q), np.arange(2 * q, 3 * q)])


def host_inputs(cfg, x, c, ctx, c_ctx, w_ada, b_ada, g_mix, g_ffn, fourier_w_out, mla_w_a, mla_g_q, mla_g_kv,
                mla_w_uq, mla_w_ukv, mla_w_o, w_gate, w_up, w_down, g_final):
    f32 = np.float32
    D, SEQ, CTX, H, QL, KVL, FF, GW = cfg["D"], cfg["SEQ"], cfg["CTX"], cfg["HEADS"], cfg["QL"], cfg["KVL"], cfg["FFN"], cfg["GRID_W"]
    T, TQ, GD = SEQ + CTX, SEQ // 4, D // 4
    A = lambda z: np.ascontiguousarray(np.asarray(z, f32))
    row = lambda z: A(z).reshape(1, -1)
    sw = _swap_idx()
    wa = A(mla_w_a[0])
    w_a_ext = np.ascontiguousarray(np.concatenate([wa, wa[:, QL + KVL + sw]], 1))
    wq = A(mla_w_uq[0])
    wq3 = wq.reshape(QL, H, QKH)
    w_uq_ext = np.ascontiguousarray(np.concatenate([wq, wq3[:, :, NOPE + sw].reshape(QL, H * ROPE)], 1))
    shared = {
        "g_fin": row(g_final), "g_q": row(mla_g_q[0]), "g_kv": row(mla_g_kv[0]), "zrow": np.zeros((1, D), f32),
        "w_out": A(fourier_w_out[0]), "w_a": w_a_ext, "w_uq": w_uq_ext, "w_ukv": A(mla_w_ukv[0]), "w_o": A(mla_w_o[0]),
        "ident": np.eye(128, dtype=f32).astype(bf16),
    }
    for i in range(2):
        shared[f"w_ada{i}"] = A(w_ada[i]); shared[f"b_ada{i}"] = row(b_ada[i])
        shared[f"g_mix{i}"] = row(g_mix[i]); shared[f"g_ffn{i}"] = row(g_ffn[i])
        shared[f"w_gate{i}"] = A(w_gate[i]); shared[f"w_up{i}"] = A(w_up[i]); shared[f"w_down{i}"] = A(w_down[i])
    k = np.arange(SEQ, dtype=np.int64)
    ang = 2.0 * np.pi * ((k[:, None] * k[None, :]) % SEQ) / SEQ
    Cn = np.cos(ang) / np.sqrt(SEQ); Sn = np.sin(ang) / np.sqrt(SEQ)
    kc = np.arange(CTX, dtype=np.int64)
    angc = 2.0 * np.pi * ((kc[:, None] * kc[None, :]) % CTX) / CTX
    Cc = np.cos(angc) / np.sqrt(CTX); Sc = np.sin(angc) / np.sqrt(CTX)
    j = np.arange(GD, dtype=np.int64)
    angm = 2.0 * np.pi * ((j[:, None] * j[None, :]) % GD) / GD
    shared["bch"] = (np.concatenate([np.cos(angm), -np.sin(angm)], 0) / np.sqrt(GD)).astype(f32).astype(bf16)
    rows_n = SEQ // GW
    rr = np.repeat(np.arange(rows_n, dtype=f32), GW); cc = np.tile(np.arange(GW, dtype=f32), rows_n)
    nf = ROPE // 4
    inv = (10000.0 ** (-np.arange(nf, dtype=f32) / nf)).astype(f32)
    angr = np.concatenate([rr[:, None] * inv, cc[:, None] * inv], -1).astype(f32)
    cs, sn = np.cos(angr).astype(f32), np.sin(angr).astype(f32)
    COS = np.concatenate([cs[:, :nf], cs[:, :nf], cs[:, nf:], cs[:, nf:]], -1)
    SIN = np.concatenate([-sn[:, :nf], sn[:, :nf], -sn[:, nf:], sn[:, nf:]], -1)
    x = np.asarray(x, f32); ctx = np.asarray(ctx, f32)
    ins = []
    apos_cache = {}
    for core in range(NCORES):
        b, q = core // 4, core % 4
        perm = np.concatenate([np.arange(q * TQ, (q + 1) * TQ)] + [np.arange(r * TQ, (r + 1) * TQ) for r in range(4) if r != q])
        d = dict(shared)
        d["xin"] = np.ascontiguousarray(np.concatenate([x[b][perm], ctx[b]], 0))
        cvv = np.zeros((128, D), f32); cvv[0] = np.asarray(c, f32)[b]; cvv[1] = np.asarray(c_ctx, f32)
        d["cv"] = cvv
        if q not in apos_cache:
            Ap = np.zeros((2 * T, T), f32)
            Ap[0:SEQ, 0:SEQ] = Cn[np.ix_(perm, perm)]
            Ap[SEQ:2 * SEQ, 0:SEQ] = Sn[np.ix_(perm, perm)]
            Ap[2 * SEQ:2 * SEQ + CTX, SEQ:] = Cc
            Ap[2 * SEQ + CTX:, SEQ:] = Sc
            apos_cache[q] = tile_at(Ap.astype(bf16))
        d["apos"] = apos_cache[q]
        d["cosq"] = np.ascontiguousarray(np.tile(COS[perm[:TQ]], (1, H)))
        d["sinq"] = np.ascontiguousarray(np.tile(SIN[perm[:TQ]], (1, H)))
        d["cosk"] = np.ascontiguousarray(np.concatenate([COS[perm], np.ones((CTX, ROPE), f32)], 0))
        d["sink"] = np.ascontiguousarray(np.concatenate([SIN[perm], np.zeros((CTX, ROPE), f32)], 0))
        ins.append(d)
    return ins


_prog_cache = {}


def run_fused(cfg, **inputs):
    key = tuple(sorted(cfg.items()))
    if key not in _prog_cache:
        _prog_cache[key] = build_fused(cfg)
    nc = _prog_cache[key]
    ins = host_inputs(cfg, **inputs)
    res = run_bass_kernel_spmd(nc, ins, core_ids=list(range(NCORES)))
    SEQ, D = cfg["SEQ"], cfg["D"]
    TQ = SEQ // 4
    out = np.zeros((cfg["BATCH"], SEQ, D), np.float32)
    for core in range(NCORES):
        b, q = core // 4, core % 4
        out[b, q * TQ:(q + 1) * TQ] = res.results[core]["out"]
    return out


def kernel(x, c, ctx, c_ctx, w_ada, b_ada, g_mix, g_ffn, fourier_w_out, mla_w_a, mla_g_q, mla_g_kv,
           mla_w_uq, mla_w_ukv, mla_w_o, w_gate, w_up, w_down, g_final):
    return run_fused(CFG_FULL, x=x, c=c, ctx=ctx, c_ctx=c_ctx, w_ada=w_ada, b_ada=b_ada, g_mix=g_mix, g_ffn=g_ffn,
                     fourier_w_out=fourier_w_out, mla_w_a=mla_w_a, mla_g_q=mla_g_q, mla_g_kv=mla_g_kv,
                     mla_w_uq=mla_w_uq, mla_w_ukv=mla_w_ukv, mla_w_o=mla_w_o, w_gate=w_gate, w_up=w_up,
                     w_down=w_down, g_final=g_final)
```

```python
import numpy as np
import ml_dtypes
from contextlib import ExitStack
import concourse.bass as bass
import concourse.mybir as mybir
from concourse.bass_utils import run_bass_kernel_spmd

F32 = mybir.dt.float32
BF16 = mybir.dt.bfloat16
AF = mybir.ActivationFunctionType
ALU = mybir.AluOpType
AX = mybir.AxisListType
bf16 = ml_dtypes.bfloat16
NCORES = 8
EPS = 1e-6
NOPE = 128
ROPE = 64
VH = 128
QKH = NOPE + ROPE
SCALE = QKH ** -0.5

CFG_FULL = dict(D=4096, SEQ=4096, CTX=256, HEADS=64, QL=1536, KVL=512, FFN=11008, GRID_W=64, BATCH=2)


class Res:
    def __init__(self, name):
        self.name = name
        self.last_w = None
        self.readers = []


class Op:
    __slots__ = ("eng", "fn", "deps", "dma", "sem", "val", "signal", "key", "grp")

    def __init__(self, eng, fn, dma, key, grp):
        self.eng = eng
        self.fn = fn
        self.deps = []
        self.dma = dma
        self.sem = None
        self.val = 0
        self.signal = dma
        self.key = key
        self.grp = grp


ENGS = ("pe", "act", "dve", "pool", "sp")
CENGS = ("pe", "act", "dve", "pool")
NGRP = 4
STQ = "pool"
ARENA = 102400


class Prog:
    def __init__(self, nc, stack):
        self.nc = nc
        self.stack = stack
        self.ops = {e: [] for e in ENGS}
        self.eng_sem = {(e, g): stack.enter_context(nc.semaphore(f"cs_{e}{g}")) for e in CENGS for g in range(NGRP)}
        self.dma_sems = {}
        self.dma_cnt = {}
        self.out_dmas = []
        self.stage = 0
        self.pending = {e: None for e in ENGS}
        self.stage_dmas = []
        self.stage_keys = {}
        self.arena = stack.enter_context(nc.sbuf_tensor("arena", [128, ARENA], BF16))
        self.off = 0

    def alloc(self, name, shape, dt):
        per = 1
        for s in shape[1:]:
            per *= s
        nb = per * (4 if dt == F32 else 2)
        nb = (nb + 63) // 64 * 64
        ne = nb // 2
        assert self.off + ne <= ARENA, (name, shape, self.off)
        v = self.arena[0:shape[0], self.off:self.off + per * (2 if dt == F32 else 1)]
        self.off += ne
        if dt == F32:
            v = v.bitcast(F32)
        if len(shape) == 3:
            v = v.rearrange("p (a b) -> p a b", b=shape[2])
        return v, Res(name)

    def barrier(self):
        deps = []
        for e in CENGS:
            for op in reversed(self.ops[e]):
                if not op.dma:
                    deps.append(op)
                    break
        deps.extend(self.stage_dmas)
        self.stage_dmas = []
        for e in ENGS:
            prev = self.pending[e] or []
            self.pending[e] = prev + deps
        self.stage += 1
        self.off = 0
        self.stage_keys = {}

    def _add(self, eng, fn, reads, writes, dma=False, key=None, chain=False):
        grp = self.stage % NGRP
        if key is not None:
            if key not in self.stage_keys:
                self.stage_keys[key] = len(self.stage_keys)
            key = f"{grp}_{self.stage_keys[key]}"
        op = Op(eng, fn, dma, key, grp)
        deps = []
        for r in reads:
            if r.last_w is not None:
                deps.append(r.last_w)
        for w in writes:
            if w.last_w is not None:
                if not (chain and w.last_w.dma and w.last_w.key == key):
                    deps.append(w.last_w)
                else:
                    deps.extend(w.last_w.deps)
            deps.extend(w.readers)
        if self.pending[eng]:
            deps.extend(self.pending[eng])
            self.pending[eng] = None
        seen = set()
        for d in deps:
            if id(d) not in seen and d is not op:
                seen.add(id(d))
                op.deps.append(d)
        for r in reads:
            r.readers.append(op)
        for w in writes:
            w.last_w = op
            w.readers = []
        if dma:
            if key not in self.dma_sems:
                self.dma_sems[key] = self.stack.enter_context(self.nc.semaphore("ds_" + key))
                self.dma_cnt[key] = 0
            self.dma_cnt[key] += 16
            op.sem = self.dma_sems[key]
            op.val = self.dma_cnt[key]
            self.stage_dmas.append(op)
        self.ops[eng].append(op)
        return op

    def op(self, eng, fn, reads=(), writes=()):
        return self._add(eng, fn, reads, writes)

    def dma(self, eng, key, out, in_, reads=(), writes=(), chain=False, is_out=False):
        op = self._add(eng, lambda e, out=out, in_=in_: e.dma_start(out=out, in_=in_), reads, writes,
                       dma=True, key=key, chain=chain)
        if is_out:
            self.out_dmas.append(op)
        return op

    def emit(self):
        fin = Op("sp", None, False, None, 0)
        fin.deps = list(self.out_dmas)
        self.ops["sp"].append(fin)
        for e in ENGS:
            for op in self.ops[e]:
                for d in op.deps:
                    if not d.dma:
                        if d.eng == "pe" and op.eng == "pe" and not op.dma:
                            continue
                        d.signal = True
        for e in CENGS:
            cnt = [0] * NGRP
            for op in self.ops[e]:
                if not op.dma and op.signal:
                    cnt[op.grp] += 1
                    op.sem = self.eng_sem[(e, op.grp)]
                    op.val = cnt[op.grp]
        nc = self.nc
        ops = self.ops

        def run(e, eng):
            seen = {}
            for op in ops[e]:
                need = {}
                for d in op.deps:
                    if not d.dma and d.eng == "pe" and e == "pe" and not op.dma:
                        continue
                    k = id(d.sem)
                    if seen.get(k, 0) >= d.val:
                        continue
                    if k not in need or need[k][1] < d.val:
                        need[k] = (d.sem, d.val)
                for k, (s, v) in need.items():
                    eng.wait_ge(s, v)
                    seen[k] = v
                if op.fn is None:
                    continue
                ins = op.fn(eng)
                if op.signal:
                    ins.then_inc(op.sem, 16 if op.dma else 1)

        with nc.Block() as block:
            @block.tensor
            def _(eng):
                run("pe", eng)

            @block.scalar
            def _(eng):
                run("act", eng)

            @block.vector
            def _(eng):
                run("dve", eng)

            @block.gpsimd
            def _(eng):
                run("pool", eng)

            @block.sync
            def _(eng):
                run("sp", eng)


def _nblocks(n, nb=512):
    return [(i, min(nb, n - i)) for i in range(0, n, nb)]


class PS:
    def __init__(self, nc, st):
        self.f = [st.enter_context(nc.psum_tensor(f"psf{i}", [128, 512], F32)) for i in range(5)]
        self.fr = [Res(f"psf{i}") for i in range(5)]
        self.t = [st.enter_context(nc.psum_tensor(f"pst{i}", [128, 4, 128], BF16)) for i in range(3)]
        self.tr = [Res(f"pst{i}") for i in range(3)]


def st_gemm(p, ps, at, MT, KC, bs, N, epi, out, r_ap=None, grow=None, brow=None):
    odt = out.dtype
    nB = len(bs)
    NB = 512
    tiled = [len(b.shape) == 4 for b in bs]
    bv = [None if tiled[i] else b.rearrange("(kc p) n -> p kc n", p=128) for i, b in enumerate(bs)]
    RES_AT = MT * KC * 256 <= 40 * 1024
    NAT = MT if RES_AT else 3
    at_bytes = NAT * KC * 256
    nbuf = 2 if (KC * NB * 2 * nB * 2 + at_bytes + 24 * 1024 <= ARENA * 2) else 1
    bsb = [[p.alloc(f"bsb{i}_{j}", [128, KC, NB], BF16) for i in range(nB)] for j in range(nbuf)]
    att = [p.alloc(f"at{i}", [128, KC, 128], BF16) for i in range(NAT)]
    if RES_AT:
        for mt in range(MT):
            p.dma("sp", "atres", att[mt][0][:, :, :], at[mt], writes=[att[mt][1]])
    ot = [p.alloc(f"ot{i}", [128, NB], odt) for i in range(2)]
    if epi != "plain":
        tmp = [p.alloc(f"tmp{i}", [128, NB], F32) for i in range(2)]
    if epi == "resid":
        rt = [p.alloc(f"rt{i}", [128, NB], F32) for i in range(2)]
        gt = [p.alloc(f"gt{i}", [128, NB], F32) for i in range(2)]
    if epi == "bias":
        bt = [p.alloc(f"bt{i}", [128, NB], F32) for i in range(2)]
    NPS = 4 // nB
    it = 0
    KG = 8
    for bi, (n0, nb) in enumerate(_nblocks(N, NB)):
        bb = bsb[bi % nbuf]
        for i in range(nB):
            for k0 in range(0, KC, KG):
                k1 = min(KC, k0 + KG)
                if tiled[i]:
                    p.dma("pool", f"b{i}_{bi % nbuf}", bb[i][0][:, k0:k1, :], bs[i][bi][:, k0:k1, :],
                          writes=[bb[i][1]], chain=(k0 > 0))
                else:
                    p.dma("pool", f"b{i}_{bi % nbuf}", bb[i][0][:, k0:k1, 0:nb], bv[i][:, k0:k1, n0:n0 + nb],
                          writes=[bb[i][1]], chain=(k0 > 0))
        if epi == "bias":
            p.dma("pool", f"bt{bi % 2}", bt[bi % 2][0][:, 0:nb], brow[0:1, n0:n0 + nb].partition_broadcast(128), writes=[bt[bi % 2][1]])
        for mt in range(MT):
            s = it % 2
            a = it % NAT
            q = it % NPS
            it += 1
            if RES_AT:
                a = mt
            else:
                p.dma("sp", f"at{a}", att[a][0][:, :, :], at[mt], writes=[att[a][1]])
            if epi == "resid":
                p.dma("act", f"rt{s}", rt[s][0][:, 0:nb], r_ap[mt * 128:(mt + 1) * 128, n0:n0 + nb], writes=[rt[s][1]])
                p.dma("act", f"gt{s}", gt[s][0][:, 0:nb], grow(mt)[0:1, n0:n0 + nb].partition_broadcast(128), writes=[gt[s][1]])
            for i in range(nB):
                z = q * nB + i

                def mm(e, i=i, a=a, z=z, nb=nb, bb=bb):
                    ins = None
                    for kc in range(KC):
                        ins = e.matmul(ps.f[z][:, 0:nb], att[a][0][:, kc, :], bb[i][0][:, kc, 0:nb],
                                       start=(kc == 0), stop=(kc == KC - 1))
                    return ins
                p.op("pe", mm, reads=[att[a][1], bb[i][1]], writes=[ps.fr[z]])
            z0 = q * nB
            if epi == "plain":
                if it % 2 == 0:
                    p.op("act", lambda e, s=s, z0=z0, nb=nb: e.activation(out=ot[s][0][:, 0:nb], in_=ps.f[z0][:, 0:nb], func=AF.Copy),
                         reads=[ps.fr[z0]], writes=[ot[s][1]])
                else:
                    p.op("dve", lambda e, s=s, z0=z0, nb=nb: e.tensor_copy(out=ot[s][0][:, 0:nb], in_=ps.f[z0][:, 0:nb]),
                         reads=[ps.fr[z0]], writes=[ot[s][1]])
            elif epi == "silu":
                p.op("act", lambda e, s=s, z0=z0, nb=nb: e.activation(out=tmp[s][0][:, 0:nb], in_=ps.f[z0][:, 0:nb], func=AF.Silu),
                     reads=[ps.fr[z0]], writes=[tmp[s][1]])
                p.op("dve", lambda e, s=s, z0=z0, nb=nb: e.tensor_tensor(out=ot[s][0][:, 0:nb], in0=tmp[s][0][:, 0:nb], in1=ps.f[z0 + 1][:, 0:nb], op=ALU.mult),
                     reads=[tmp[s][1], ps.fr[z0 + 1]], writes=[ot[s][1]])
            elif epi == "resid":
                p.op("dve", lambda e, s=s, z0=z0, nb=nb: e.tensor_tensor(out=tmp[s][0][:, 0:nb], in0=ps.f[z0][:, 0:nb], in1=gt[s][0][:, 0:nb], op=ALU.mult),
                     reads=[ps.fr[z0], gt[s][1]], writes=[tmp[s][1]])
                p.op("pool", lambda e, s=s, nb=nb: e.tensor_tensor(out=ot[s][0][:, 0:nb], in0=tmp[s][0][:, 0:nb], in1=rt[s][0][:, 0:nb], op=ALU.add),
                     reads=[tmp[s][1], rt[s][1]], writes=[ot[s][1]])
            else:
                p.op("dve", lambda e, s=s, z0=z0, nb=nb, bi=bi: e.tensor_tensor(out=ot[s][0][:, 0:nb], in0=ps.f[z0][:, 0:nb], in1=bt[bi % 2][0][:, 0:nb], op=ALU.add),
                     reads=[ps.fr[z0], bt[bi % 2][1]], writes=[ot[s][1]])
            p.dma("sp" if RES_AT else STQ, f"ot{s}", out[mt * 128:(mt + 1) * 128, n0:n0 + nb], ot[s][0][:, 0:nb], reads=[ot[s][1]])
    p.barrier()


def st_trans(p, ps, ident, src_fn, MT, K, at_out):
    KC = K // 128
    xt = [p.alloc(f"xt{i}", [128, K], BF16) for i in range(2)]
    att = [p.alloc(f"att{i}", [128, KC, 128], BF16) for i in range(2)]
    iT = 0
    for mt in range(MT):
        s = mt % 2
        for j, (c0, ap) in enumerate(src_fn(mt)):
            w = ap.shape[1]
            p.dma("sp" if j % 2 == 0 else "act", f"xt{s}", xt[s][0][:, c0:c0 + w], ap, writes=[xt[s][1]], chain=(j > 0))
        for c0 in range(0, KC, 4):
            c1 = min(KC, c0 + 4)
            z = iT % 3
            iT += 1

            def tr(e, s=s, z=z, c0=c0, c1=c1):
                ins = None
                for c in range(c0, c1):
                    ins = e.transpose(out=ps.t[z][:, c - c0, :], in_=xt[s][0][:, c * 128:(c + 1) * 128], identity=ident[0][:, :])
                return ins
            p.op("pe", tr, reads=[xt[s][1], ident[1]], writes=[ps.tr[z]])
            if (c0 // 4) % 2 == 0:
                p.op("dve", lambda e, s=s, z=z, c0=c0, c1=c1: e.tensor_copy(out=att[s][0][:, c0:c1, :], in_=ps.t[z][:, 0:c1 - c0, :]),
                     reads=[ps.tr[z]], writes=[att[s][1]])
            else:
                p.op("act", lambda e, s=s, z=z, c0=c0, c1=c1: e.activation(out=att[s][0][:, c0:c1, :], in_=ps.t[z][:, 0:c1 - c0, :], func=AF.Copy),
                     reads=[ps.tr[z]], writes=[att[s][1]])
        p.dma("pool", f"att{s}", at_out[mt], att[s][0][:, :, :], reads=[att[s][1]])
    p.barrier()


def st_norm(p, x_fn, NT, D, g_row, sc_fn, sh_fn, out_fn, out_dt, final=False):
    gb = p.alloc("gb", [128, D], F32)
    xt = [p.alloc(f"xt{i}", [128, D], F32) for i in range(2)]
    sct = [p.alloc(f"sct{i}", [128, D], F32) for i in range(2)]
    sht = [p.alloc(f"sht{i}", [128, D], F32) for i in range(2)]
    sq = p.alloc("sq", [128, D], F32)
    ss = [p.alloc(f"ss{i}", [128, 4], F32) for i in range(2)]
    ot = [p.alloc(f"ot{i}", [128, D], out_dt) for i in range(2)]
    p.dma("pool", "gb", gb[0][:, :], g_row.partition_broadcast(128), writes=[gb[1]])
    for t in range(NT):
        s = t % 2
        p.dma("sp", f"xt{s}", xt[s][0][:, :], x_fn(t), writes=[xt[s][1]])
        p.dma("pool", f"sct{s}", sct[s][0][:, :], sc_fn(t).partition_broadcast(128), writes=[sct[s][1]])
        p.dma("pool", f"sht{s}", sht[s][0][:, :], sh_fn(t).partition_broadcast(128), writes=[sht[s][1]])
        p.op("act", lambda e, s=s: e.activation(out=sq[0][:, :], in_=xt[s][0][:, :], func=AF.Square, accum_out=ss[s][0][:, 0:1]),
             reads=[xt[s][1]], writes=[sq[1], ss[s][1]])
        p.op("dve", lambda e, s=s: e.tensor_scalar(out=ss[s][0][:, 1:2], in0=ss[s][0][:, 0:1], scalar1=1.0 / D, scalar2=EPS, op0=ALU.mult, op1=ALU.add),
             reads=[ss[s][1]], writes=[ss[s][1]])
        p.op("act", lambda e, s=s: e.activation(out=ss[s][0][:, 2:3], in_=ss[s][0][:, 1:2], func=AF.Sqrt),
             reads=[ss[s][1]], writes=[ss[s][1]])
        p.op("dve", lambda e, s=s: e.reciprocal(out=ss[s][0][:, 3:4], in_=ss[s][0][:, 2:3]),
             reads=[ss[s][1]], writes=[ss[s][1]])
        p.op("dve", lambda e, s=s: e.scalar_tensor_tensor(out=sct[s][0][:, :], in0=sct[s][0][:, :], scalar=1.0, in1=gb[0][:, :], op0=ALU.add, op1=ALU.mult),
             reads=[sct[s][1], gb[1]], writes=[sct[s][1]])
        p.op("dve", lambda e, s=s: e.scalar_tensor_tensor(out=xt[s][0][:, :], in0=xt[s][0][:, :], scalar=ss[s][0][:, 3:4], in1=sct[s][0][:, :], op0=ALU.mult, op1=ALU.mult),
             reads=[xt[s][1], ss[s][1], sct[s][1]], writes=[xt[s][1]])
        p.op("pool", lambda e, s=s: e.tensor_tensor(out=ot[s][0][:, :], in0=xt[s][0][:, :], in1=sht[s][0][:, :], op=ALU.add),
             reads=[xt[s][1], sht[s][1]], writes=[ot[s][1]])
        p.dma("pool", f"ot{s}", out_fn(t), ot[s][0][:, :], reads=[ot[s][1]], is_out=final)
    p.barrier()


def st_silu(p, a_ap, out_ap, Fd):
    at = p.alloc("sa", [128, Fd], F32)
    ot = p.alloc("so", [128, Fd], BF16)
    p.dma("sp", "sa", at[0][:, :], a_ap, writes=[at[1]])
    p.op("act", lambda e: e.activation(out=ot[0][:, :], in_=at[0][:, :], func=AF.Silu), reads=[at[1]], writes=[ot[1]])
    p.dma("sp", "so", out_ap, ot[0][:, :], reads=[ot[1]])
    p.barrier()


def st_muladd(p, NT, shape, a_dt, a_fn, b_fn, c_fn, d_fn, out_fn):
    dts = [a_dt, F32, a_dt, F32]
    fns = [a_fn, b_fn, c_fn, d_fn]
    names = "abcd"
    tl = [[p.alloc(f"{names[j]}{i}", shape, dts[j]) for i in range(2)] for j in range(4)]
    t1 = [p.alloc(f"t1{i}", shape, F32) for i in range(2)]
    t2 = [p.alloc(f"t2{i}", shape, F32) for i in range(2)]
    ot = [p.alloc(f"ot{i}", shape, BF16) for i in range(2)]
    for t in range(NT):
        s = t % 2
        for j in range(4):
            p.dma("sp" if j % 2 == 0 else "pool", f"{names[j]}{s}", tl[j][s][0], fns[j](t), writes=[tl[j][s][1]])
        p.op("dve", lambda e, s=s: e.tensor_tensor(out=t1[s][0], in0=tl[0][s][0], in1=tl[1][s][0], op=ALU.mult),
             reads=[tl[0][s][1], tl[1][s][1]], writes=[t1[s][1]])
        p.op("pool", lambda e, s=s: e.tensor_tensor(out=t2[s][0], in0=tl[2][s][0], in1=tl[3][s][0], op=ALU.mult),
             reads=[tl[2][s][1], tl[3][s][1]], writes=[t2[s][1]])
        p.op("dve", lambda e, s=s: e.tensor_tensor(out=ot[s][0], in0=t1[s][0], in1=t2[s][0], op=ALU.add),
             reads=[t1[s][1], t2[s][1]], writes=[ot[s][1]])
        p.dma("act", f"ot{s}", out_fn(t), ot[s][0], reads=[ot[s][1]])
    p.barrier()


def st_attn(p, ps, ident, H, NQ, NK, qx, qrot, kvx, krot, o):
    NKC = NK // 128
    NQT = NQ // 128
    kbl = _nblocks(NK, 512)
    krtok = p.alloc("krtok", [128, NKC, ROPE], BF16)
    krT = p.alloc("krT", [64, NK], BF16)
    qtok = [p.alloc(f"qtok{i}", [128, NQT, NOPE], BF16) for i in range(2)]
    qrtok = [p.alloc(f"qrtok{i}", [128, NQT, ROPE], BF16) for i in range(2)]
    ktok = [p.alloc(f"ktok{i}", [128, NKC, NOPE], BF16) for i in range(2)]
    vt = [p.alloc(f"vt{i}", [128, NKC, VH], BF16) for i in range(2)]
    qn = [p.alloc(f"qn{i}", [128, NQ], BF16) for i in range(2)]
    qr_ = [p.alloc(f"qr{i}", [64, NQ], BF16) for i in range(2)]
    kn = [p.alloc(f"kn{i}", [128, NK], BF16) for i in range(2)]
    S = [p.alloc(f"S{i}", [128, NK], F32) for i in range(2)]
    P = [p.alloc(f"P{i}", [128, NK], BF16) for i in range(2)]
    PT = [p.alloc(f"PT{i}", [128, NKC, 128], BF16) for i in range(2)]
    st4 = [p.alloc(f"st{i}", [128, 4], F32) for i in range(2)]
    ot = [p.alloc(f"ot{i}", [128, 128], BF16) for i in range(2)]
    cnt = {"S": 0, "T": 0, "q": 0, "e": 0}

    def trans_into(src, dst, nchunk, rows_out):
        for c0 in range(0, nchunk, 4):
            c1 = min(nchunk, c0 + 4)
            z = cnt["T"] % 3
            cnt["T"] += 1

            def tr(e, z=z, c0=c0, c1=c1):
                ins = None
                for c in range(c0, c1):
                    ins = e.transpose(out=ps.t[z][0:rows_out, c - c0, :], in_=src[0][:, c, :], identity=ident[0][:, :])
                return ins
            p.op("pe", tr, reads=[src[1], ident[1]], writes=[ps.tr[z]])
            cnt["e"] += 1
            dv = dst[0][:, c0 * 128:c1 * 128].rearrange("p (a b) -> p a b", b=128)
            if cnt["e"] % 2 == 0:
                p.op("dve", lambda e, z=z, c0=c0, c1=c1, dv=dv: e.tensor_copy(out=dv, in_=ps.t[z][0:rows_out, 0:c1 - c0, :]),
                     reads=[ps.tr[z]], writes=[dst[1]])
            else:
                p.op("act", lambda e, z=z, c0=c0, c1=c1, dv=dv: e.activation(out=dv, in_=ps.t[z][0:rows_out, 0:c1 - c0, :], func=AF.Copy),
                     reads=[ps.tr[z]], writes=[dst[1]])

    p.dma("sp", "krtok", krtok[0][:, :, :], krot.rearrange("(c p) d -> p c d", p=128), writes=[krtok[1]])
    trans_into(krtok, krT, NKC, ROPE)
    def prologue(h):
        b = h % 2
        p.dma("sp", f"qtok{b}", qtok[b][0][:, :, :], qx[:, h * QKH:h * QKH + NOPE].rearrange("(c p) d -> p c d", p=128), writes=[qtok[b][1]])
        p.dma("sp", f"qrtok{b}", qrtok[b][0][:, :, :], qrot[:, h * ROPE:(h + 1) * ROPE].rearrange("(c p) d -> p c d", p=128), writes=[qrtok[b][1]])
        p.dma("sp", f"ktok{b}", ktok[b][0][:, :, :], kvx[:, h * 256:h * 256 + NOPE].rearrange("(c p) d -> p c d", p=128), writes=[ktok[b][1]])
        p.dma("act", f"vt{b}", vt[b][0][:, :, :], kvx[:, h * 256 + NOPE:(h + 1) * 256].rearrange("(c p) d -> p c d", p=128), writes=[vt[b][1]])
        trans_into(ktok[b], kn[b], NKC, 128)
        trans_into(qtok[b], qn[b], NQT, 128)
        trans_into(qrtok[b], qr_[b], NQT, ROPE)

    def phase_a(idx, h, qt):
        b = h % 2
        s = idx % 2
        q0 = qt * 128
        for bi, (k0, kb) in enumerate(kbl):
            z = cnt["S"] % 3
            cnt["S"] += 1

            def mmS(e, b=b, z=z, q0=q0, k0=k0, kb=kb):
                e.matmul(ps.f[z][:, 0:kb], qn[b][0][:, q0:q0 + 128], kn[b][0][:, k0:k0 + kb], start=True, stop=False)
                return e.matmul(ps.f[z][:, 0:kb], qr_[b][0][:, q0:q0 + 128], krT[0][:, k0:k0 + kb], start=False, stop=True)
            p.op("pe", mmS, reads=[qn[b][1], qr_[b][1], kn[b][1], krT[1]], writes=[ps.fr[z]])
            if bi % 2 == 0:
                p.op("dve", lambda e, s=s, z=z, k0=k0, kb=kb: e.tensor_copy(out=S[s][0][:, k0:k0 + kb], in_=ps.f[z][:, 0:kb]),
                     reads=[ps.fr[z]], writes=[S[s][1]])
            else:
                p.op("act", lambda e, s=s, z=z, k0=k0, kb=kb: e.activation(out=S[s][0][:, k0:k0 + kb], in_=ps.f[z][:, 0:kb], func=AF.Copy),
                     reads=[ps.fr[z]], writes=[S[s][1]])
        p.op("dve", lambda e, s=s: e.reduce_max(out=st4[s][0][:, 0:1], in_=S[s][0][:, :], axis=AX.X),
             reads=[S[s][1]], writes=[st4[s][1]])
        p.op("dve", lambda e, s=s: e.tensor_scalar(out=st4[s][0][:, 1:2], in0=st4[s][0][:, 0:1], scalar1=-SCALE, scalar2=None, op0=ALU.mult),
             reads=[st4[s][1]], writes=[st4[s][1]])
        p.op("act", lambda e, s=s: e.activation(out=P[s][0][:, :], in_=S[s][0][:, :], func=AF.Exp, bias=st4[s][0][:, 1:2], scale=SCALE, accum_out=st4[s][0][:, 2:3]),
             reads=[S[s][1], st4[s][1]], writes=[P[s][1], st4[s][1]])

    def phase_b(idx, h, qt):
        b = h % 2
        s = idx % 2
        q0 = qt * 128
        Pv = (P[s][0].rearrange("p (a b) -> p a b", b=128), P[s][1])
        PTv = (PT[s][0].rearrange("p a b -> p (a b)"), PT[s][1])
        trans_into(Pv, PTv, NKC, 128)
        zo = 3 + (idx % 2)

        def mmO(e, s=s, b=b, zo=zo):
            ins = None
            for c in range(NKC):
                ins = e.matmul(ps.f[zo][:, 0:128], PT[s][0][:, c, :], vt[b][0][:, c, :], start=(c == 0), stop=(c == NKC - 1))
            return ins
        p.op("pe", mmO, reads=[PT[s][1], vt[b][1]], writes=[ps.fr[zo]])
        p.op("dve", lambda e, s=s: e.reciprocal(out=st4[s][0][:, 3:4], in_=st4[s][0][:, 2:3]),
             reads=[st4[s][1]], writes=[st4[s][1]])
        p.op("act", lambda e, s=s, zo=zo: e.activation(out=ot[s][0][:, :], in_=ps.f[zo][:, 0:128], func=AF.Copy, scale=st4[s][0][:, 3:4]),
             reads=[ps.fr[zo], st4[s][1]], writes=[ot[s][1]])
        p.dma("pool", f"ot{s}", o[q0:q0 + 128, h * VH:(h + 1) * VH], ot[s][0][:, :], reads=[ot[s][1]])

    items = [(h, qt) for h in range(H) for qt in range(NQT)]
    for idx, (h, qt) in enumerate(items):
        if idx == 0:
            prologue(h)
            phase_a(0, h, qt)
        if idx + 1 < len(items):
            h1, qt1 = items[idx + 1]
            if qt1 == 0:
                prologue(h1)
            phase_a(idx + 1, h1, qt1)
        phase_b(idx, h, qt)
    p.barrier()


def build_fused(cfg, debug=False):
    D, SEQ, CTX, H, QL, KVL, FF = cfg["D"], cfg["SEQ"], cfg["CTX"], cfg["HEADS"], cfg["QL"], cfg["KVL"], cfg["FFN"]
    T = SEQ + CTX
    TQ = SEQ // 4
    MTt, MTq = T // 128, TQ // 128
    NLt = SEQ // 128
    GD = D // 4
    DC = D // 128
    NWA = QL + KVL + 2 * ROPE
    NQX = H * QKH + H * ROPE
    nc = bass.Bass("TRN2", target_bir_lowering=False)
    I = lambda name, shape, dt: nc.dram_tensor(name, shape, dt, kind="ExternalInput").ap()
    W = lambda name, shape, dt=BF16: nc.dram_tensor(name, shape, dt, kind=("ExternalOutput" if debug else "Internal")).ap()
    xin = I("xin", [T, D], F32)
    cv = I("cv", [128, D], F32)
    w_ada = [I(f"w_ada{i}", [-(-(6 * D) // 512), 128, (D) // 128, 512], F32) for i in range(2)]
    b_ada = [I(f"b_ada{i}", [1, 6 * D], F32) for i in range(2)]
    g_mix = [I(f"g_mix{i}", [1, D], F32) for i in range(2)]
    g_ffn = [I(f"g_ffn{i}", [1, D], F32) for i in range(2)]
    g_fin = I("g_fin", [1, D], F32)
    g_q = I("g_q", [1, QL], F32)
    g_kv = I("g_kv", [1, KVL], F32)
    zrow = I("zrow", [1, D], F32)
    w_out = I("w_out", [-(-(D) // 512), 128, (D) // 128, 512], F32)
    w_a = I("w_a", [-(-(NWA) // 512), 128, (D) // 128, 512], F32)
    w_uq = I("w_uq", [-(-(NQX) // 512), 128, (QL) // 128, 512], F32)
    w_ukv = I("w_ukv", [-(-(H * 256) // 512), 128, (KVL) // 128, 512], F32)
    w_o = I("w_o", [-(-(D) // 512), 128, (H * VH) // 128, 512], F32)
    w_gate = [I(f"w_gate{i}", [-(-(FF) // 512), 128, (D) // 128, 512], F32) for i in range(2)]
    w_up = [I(f"w_up{i}", [-(-(FF) // 512), 128, (D) // 128, 512], F32) for i in range(2)]
    w_down = [I(f"w_down{i}", [-(-(D) // 512), 128, (FF) // 128, 512], F32) for i in range(2)]
    apos = I("apos", [2 * T // 128, 128, MTt, 128], BF16)
    bch = I("bch", [2 * GD, GD], BF16)
    ident_in = I("ident", [128, 128], BF16)
    cosq = I("cosq", [TQ, H * ROPE], F32)
    sinq = I("sinq", [TQ, H * ROPE], F32)
    cosk = I("cosk", [T, ROPE], F32)
    sink = I("sink", [T, ROPE], F32)
    out = nc.dram_tensor("out", [TQ, D], F32, kind="ExternalOutput").ap()
    s_act = W("s_act", [128, D]); s_at = W("s_at", [1, 128, DC, 128])
    mod = [W(f"mod{i}", [128, 6 * D], F32) for i in range(2)]
    Hh = W("Hh", [T, D])
    Pp = W("Pp", [2 * T, D])
    a2_at = W("a2_at", [MTt, 128, 2 * GD // 128, 128])
    Fa = W("Fa", [T, D]); F_at = W("F_at", [MTt, 128, DC, 128])
    X1 = W("X1", [T, D], F32); X2 = W("X2", [T, D], F32)
    H_at = W("H_at", [MTt, 128, DC, 128])
    aa = W("aa", [T, FF]); a_at = W("a_at", [MTt, 128, FF // 128, 128])
    ca = W("ca", [T, NWA], F32)
    cqn = W("cqn", [TQ, QL]); cqn_at = W("cqn_at", [MTq, 128, QL // 128, 128])
    ckvn = W("ckvn", [T, KVL]); ckvn_at = W("ckvn_at", [MTt, 128, KVL // 128, 128])
    qx = W("qx", [TQ, NQX]); kvx = W("kvx", [T, H * 256])
    qrot = W("qrot", [TQ, H * ROPE]); krot = W("krot", [T, ROPE])
    oo = W("oo", [TQ, H * VH]); o_at = W("o_at", [MTq, 128, H * VH // 128, 128])
    X3 = W("X3", [TQ, D], F32); X4 = W("X4", [TQ, D], F32)

    with ExitStack() as st:
        p = Prog(nc, st)
        ps = PS(nc, st)
        idt = st.enter_context(nc.sbuf_tensor("ident_sb", [128, 128], BF16))
        ident = (idt, Res("ident"))
        p.dma("sp", "ident", idt[:, :], ident_in[:, :], writes=[ident[1]])
        grp = lambda mt: 0 if mt < NLt else 1

        def modrow(i, j):
            return lambda mt: mod[i][grp(mt):grp(mt) + 1, j * D:(j + 1) * D]

        def rows(ap, c0=0, c1=None):
            return lambda mt: [(0, ap[mt * 128:(mt + 1) * 128, c0:(c1 if c1 is not None else ap.shape[1])])]

        st_silu(p, cv[:, :], s_act[:, :], D)
        st_trans(p, ps, ident, rows(s_act), 1, D, s_at)
        for i in range(2):
            st_gemm(p, ps, s_at, 1, DC, [w_ada[i]], 6 * D, "bias", mod[i], brow=b_ada[i])

        def norm_mod(Xs, nt, i, jsh, jsc, g_row, out_ap):
            st_norm(p, lambda t: Xs[t * 128:(t + 1) * 128, :], nt, D, g_row,
                    lambda t: modrow(i, jsc)(t), lambda t: modrow(i, jsh)(t),
                    lambda t: out_ap[t * 128:(t + 1) * 128, :], BF16)

        def ffn(Xs, Xd, nt, i):
            norm_mod(Xs, nt, i, 3, 4, g_ffn[i][0:1, :], Hh)
            st_trans(p, ps, ident, rows(Hh), nt, D, H_at)
            st_gemm(p, ps, H_at, nt, DC, [w_gate[i], w_up[i]], FF, "silu", aa)
            st_trans(p, ps, ident, rows(aa), nt, FF, a_at)
            st_gemm(p, ps, a_at, nt, FF // 128, [w_down[i]], D, "resid", Xd, r_ap=Xs, grow=modrow(i, 5))

        norm_mod(xin, MTt, 0, 0, 1, g_mix[0][0:1, :], Hh)
        st_gemm(p, ps, apos, 2 * MTt, MTt, [Hh], D, "plain", Pp)
        for g in range(4):
            def a2src(mt, g=g):
                if mt < NLt:
                    r0, r1 = mt * 128, SEQ + mt * 128
                else:
                    r0, r1 = 2 * SEQ + (mt - NLt) * 128, 2 * SEQ + CTX + (mt - NLt) * 128
                return [(0, Pp[r0:r0 + 128, g * GD:(g + 1) * GD]), (GD, Pp[r1:r1 + 128, g * GD:(g + 1) * GD])]
            st_trans(p, ps, ident, a2src, MTt, 2 * GD, a2_at)
            st_gemm(p, ps, a2_at, MTt, 2 * GD // 128, [bch], GD, "plain", Fa[:, g * GD:(g + 1) * GD])
        st_trans(p, ps, ident, rows(Fa), MTt, D, F_at)
        st_gemm(p, ps, F_at, MTt, DC, [w_out], D, "resid", X1, r_ap=xin, grow=modrow(0, 2))
        ffn(X1, X2, MTt, 0)

        norm_mod(X2, MTt, 1, 0, 1, g_mix[1][0:1, :], Hh)
        st_trans(p, ps, ident, rows(Hh), MTt, D, H_at)
        st_gemm(p, ps, H_at, MTt, DC, [w_a], NWA, "plain", ca)
        zr = lambda n: (lambda t: zrow[0:1, 0:n])
        st_norm(p, lambda t: ca[t * 128:(t + 1) * 128, 0:QL], MTq, QL, g_q[0:1, :], zr(QL), zr(QL),
                lambda t: cqn[t * 128:(t + 1) * 128, :], BF16)
        st_norm(p, lambda t: ca[t * 128:(t + 1) * 128, QL:QL + KVL], MTt, KVL, g_kv[0:1, :], zr(KVL), zr(KVL),
                lambda t: ckvn[t * 128:(t + 1) * 128, :], BF16)
        st_trans(p, ps, ident, rows(cqn), MTq, QL, cqn_at)
        st_trans(p, ps, ident, rows(ckvn), MTt, KVL, ckvn_at)
        st_gemm(p, ps, cqn_at, MTq, QL // 128, [w_uq], NQX, "plain", qx)
        st_gemm(p, ps, ckvn_at, MTt, KVL // 128, [w_ukv], H * 256, "plain", kvx)
        rs = lambda t: slice(t * 128, (t + 1) * 128)
        h3 = lambda ap: ap.rearrange("t (h d) -> t h d", d=ROPE)
        st_muladd(p, MTq, [128, H, ROPE], BF16,
                  lambda t: qx[rs(t), 0:H * QKH].rearrange("t (h d) -> t h d", d=QKH)[:, :, NOPE:QKH],
                  lambda t: h3(cosq[rs(t), :]),
                  lambda t: h3(qx[rs(t), H * QKH:NQX]),
                  lambda t: h3(sinq[rs(t), :]),
                  lambda t: h3(qrot[rs(t), :]))
        st_muladd(p, MTt, [128, ROPE], F32,
                  lambda t: ca[rs(t), QL + KVL:QL + KVL + ROPE], lambda t: cosk[rs(t), :],
                  lambda t: ca[rs(t), QL + KVL + ROPE:NWA], lambda t: sink[rs(t), :],
                  lambda t: krot[rs(t), :])
        st_attn(p, ps, ident, H, TQ, T, qx, qrot, kvx, krot, oo)
        st_trans(p, ps, ident, rows(oo), MTq, H * VH, o_at)
        st_gemm(p, ps, o_at, MTq, H * VH // 128, [w_o], D, "resid", X3, r_ap=X2, grow=modrow(1, 2))
        ffn(X3, X4, MTq, 1)
        st_norm(p, lambda t: X4[t * 128:(t + 1) * 128, :], MTq, D, g_fin[0:1, :], zr(D), zr(D),
                lambda t: out[t * 128:(t + 1) * 128, :], F32, final=True)
        p.emit()
    return nc


def tile_at(A):
    M, K = A.shape
    return np.ascontiguousarray(A.reshape(M // 128, 128, K // 128, 128).transpose(0, 3, 2, 1))


def tile_b(W):
    K, N = W.shape
    nblk = -(-N // 512)
    if nblk * 512 != N:
        W = np.concatenate([W, np.zeros((K, nblk * 512 - N), W.dtype)], 1)
    return np.ascontiguousarray(W.reshape(K // 128, 128, nblk, 512).transpose(2, 1, 0, 3))


def _swap_idx():
    q = ROPE // 4
    return np.concatenate([np.arange(q, 2 * q), np.arange(0, q), np.arange(3 * q, 4 * q), np.arange(2 * q, 3 * q)])


def host_inputs(cfg, x, c, ctx, c_ctx, w_ada, b_ada, g_mix, g_ffn, fourier_w_out, mla_w_a, mla_g_q, mla_g_kv,
                mla_w_uq, mla_w_ukv, mla_w_o, w_gate, w_up, w_down, g_final):
    f32 = np.float32
    D, SEQ, CTX, H, QL, KVL, FF, GW = cfg["D"], cfg["SEQ"], cfg["CTX"], cfg["HEADS"], cfg["QL"], cfg["KVL"], cfg["FFN"], cfg["GRID_W"]
    T, TQ, GD = SEQ + CTX, SEQ // 4, D // 4
    A = lambda z: np.ascontiguousarray(np.asarray(z, f32))
    row = lambda z: A(z).reshape(1, -1)
    sw = _swap_idx()
    wa = A(mla_w_a[0])
    w_a_ext = np.ascontiguousarray(np.concatenate([wa, wa[:, QL + KVL + sw]], 1))
    wq = A(mla_w_uq[0])
    wq3 = wq.reshape(QL, H, QKH)
    w_uq_ext = np.ascontiguousarray(np.concatenate([wq, wq3[:, :, NOPE + sw].reshape(QL, H * ROPE)], 1))
    shared = {
        "g_fin": row(g_final), "g_q": row(mla_g_q[0]), "g_kv": row(mla_g_kv[0]), "zrow": np.zeros((1, D), f32),
        "w_out": tile_b(A(fourier_w_out[0])), "w_a": tile_b(w_a_ext), "w_uq": tile_b(w_uq_ext), "w_ukv": tile_b(A(mla_w_ukv[0])), "w_o": tile_b(A(mla_w_o[0])),
        "ident": np.eye(128, dtype=f32).astype(bf16),
    }
    for i in range(2):
        shared[f"w_ada{i}"] = tile_b(A(w_ada[i])); shared[f"b_ada{i}"] = row(b_ada[i])
        shared[f"g_mix{i}"] = row(g_mix[i]); shared[f"g_ffn{i}"] = row(g_ffn[i])
        shared[f"w_gate{i}"] = tile_b(A(w_gate[i])); shared[f"w_up{i}"] = tile_b(A(w_up[i])); shared[f"w_down{i}"] = tile_b(A(w_down[i]))
    k = np.arange(SEQ, dtype=np.int64)
    ang = 2.0 * np.pi * ((k[:, None] * k[None, :]) % SEQ) / SEQ
    Cn = np.cos(ang) / np.sqrt(SEQ); Sn = np.sin(ang) / np.sqrt(SEQ)
    kc = np.arange(CTX, dtype=np.int64)
    angc = 2.0 * np.pi * ((kc[:, None] * kc[None, :]) % CTX) / CTX
    Cc = np.cos(angc) / np.sqrt(CTX); Sc = np.sin(angc) / np.sqrt(CTX)
    j = np.arange(GD, dtype=np.int64)
    angm = 2.0 * np.pi * ((j[:, None] * j[None, :]) % GD) / GD
    shared["bch"] = (np.concatenate([np.cos(angm), -np.sin(angm)], 0) / np.sqrt(GD)).astype(f32).astype(bf16)
    rows_n = SEQ // GW
    rr = np.repeat(np.arange(rows_n, dtype=f32), GW); cc = np.tile(np.arange(GW, dtype=f32), rows_n)
    nf = ROPE // 4
    inv = (10000.0 ** (-np.arange(nf, dtype=f32) / nf)).astype(f32)
    angr = np.concatenate([rr[:, None] * inv, cc[:, None] * inv], -1).astype(f32)
    cs, sn = np.cos(angr).astype(f32), np.sin(angr).astype(f32)
    COS = np.concatenate([cs[:, :nf], cs[:, :nf], cs[:, nf:], cs[:, nf:]], -1)
    SIN = np.concatenate([-sn[:, :nf], sn[:, :nf], -sn[:, nf:], sn[:, nf:]], -1)
    x = np.asarray(x, f32); ctx = np.asarray(ctx, f32)
    ins = []
    apos_cache = {}
    for core in range(NCORES):
        b, q = core // 4, core % 4
        perm = np.concatenate([np.arange(q * TQ, (q + 1) * TQ)] + [np.arange(r * TQ, (r + 1) * TQ) for r in range(4) if r != q])
        d = dict(shared)
        d["xin"] = np.ascontiguousarray(np.concatenate([x[b][perm], ctx[b]], 0))
        cvv = np.zeros((128, D), f32); cvv[0] = np.asarray(c, f32)[b]; cvv[1] = np.asarray(c_ctx, f32)
        d["cv"] = cvv
        if q not in apos_cache:
            Ap = np.zeros((2 * T, T), f32)
            Ap[0:SEQ, 0:SEQ] = Cn[np.ix_(perm, perm)]
            Ap[SEQ:2 * SEQ, 0:SEQ] = Sn[np.ix_(perm, perm)]
            Ap[2 * SEQ:2 * SEQ + CTX, SEQ:] = Cc
            Ap[2 * SEQ + CTX:, SEQ:] = Sc
            apos_cache[q] = tile_at(Ap.astype(bf16))
        d["apos"] = apos_cache[q]
        d["cosq"] = np.ascontiguousarray(np.tile(COS[perm[:TQ]], (1, H)))
        d["sinq"] = np.ascontiguousarray(np.tile(SIN[perm[:TQ]], (1, H)))
        d["cosk"] = np.ascontiguousarray(np.concatenate([COS[perm], np.ones((CTX, ROPE), f32)], 0))
        d["sink"] = np.ascontiguousarray(np.concatenate([SIN[perm], np.zeros((CTX, ROPE), f32)], 0))
        ins.append(d)
    return ins


_prog_cache = {}


def run_fused(cfg, **inputs):
    key = tuple(sorted(cfg.items()))
    if key not in _prog_cache:
        _prog_cache[key] = build_fused(cfg)
    nc = _prog_cache[key]
    ins = host_inputs(cfg, **inputs)
    res = run_bass_kernel_spmd(nc, ins, core_ids=list(range(NCORES)))
    SEQ, D = cfg["SEQ"], cfg["D"]
    TQ = SEQ // 4
    out = np.zeros((cfg["BATCH"], SEQ, D), np.float32)
    for core in range(NCORES):
        b, q = core // 4, core % 4
        out[b, q * TQ:(q + 1) * TQ] = res.results[core]["out"]
    return out


def kernel(x, c, ctx, c_ctx, w_ada, b_ada, g_mix, g_ffn, fourier_w_out, mla_w_a, mla_g_q, mla_g_kv,
           mla_w_uq, mla_w_ukv, mla_w_o, w_gate, w_up, w_down, g_final):
    return run_fused(CFG_FULL, x=x, c=c, ctx=ctx, c_ctx=c_ctx, w_ada=w_ada, b_ada=b_ada, g_mix=g_mix, g_ffn=g_ffn,
                     fourier_w_out=fourier_w_out, mla_w_a=mla_w_a, mla_g_q=mla_g_q, mla_g_kv=mla_g_kv,
                     mla_w_uq=mla_w_uq, mla_w_ukv=mla_w_ukv, mla_w_o=mla_w_o, w_gate=w_gate, w_up=w_up,
                     w_down=w_down, g_final=g_final)
```

```python
import numpy as np
import ml_dtypes
from contextlib import ExitStack
import concourse.bass as bass
import concourse.mybir as mybir
from concourse.bass_utils import run_bass_kernel_spmd

F32 = mybir.dt.float32
BF16 = mybir.dt.bfloat16
AF = mybir.ActivationFunctionType
ALU = mybir.AluOpType
AX = mybir.AxisListType
bf16 = ml_dtypes.bfloat16
NCORES = 8
EPS = 1e-6
NOPE = 128
ROPE = 64
VH = 128
QKH = NOPE + ROPE
SCALE = QKH ** -0.5

CFG_FULL = dict(D=4096, SEQ=4096, CTX=256, HEADS=64, QL=1536, KVL=512, FFN=11008, GRID_W=64, BATCH=2)


class Res:
    def __init__(self, name):
        self.name = name
        self.last_w = None
        self.readers = []


class Op:
    __slots__ = ("eng", "fn", "deps", "dma", "sem", "val", "signal", "key", "grp")

    def __init__(self, eng, fn, dma, key, grp):
        self.eng = eng
        self.fn = fn
        self.deps = []
        self.dma = dma
        self.sem = None
        self.val = 0
        self.signal = dma
        self.key = key
        self.grp = grp


ENGS = ("pe", "act", "dve", "pool", "sp")
CENGS = ("pe", "act", "dve", "pool")
NGRP = 4
STQ = "pool"
ARENA = 102400


class Prog:
    def __init__(self, nc, stack):
        self.nc = nc
        self.stack = stack
        self.ops = {e: [] for e in ENGS}
        self.eng_sem = {(e, g): stack.enter_context(nc.semaphore(f"cs_{e}{g}")) for e in CENGS for g in range(NGRP)}
        self.dma_sems = {}
        self.dma_cnt = {}
        self.out_dmas = []
        self.stage = 0
        self.pending = {e: None for e in ENGS}
        self.stage_dmas = []
        self.stage_keys = {}
        self.arena = stack.enter_context(nc.sbuf_tensor("arena", [128, ARENA], BF16))
        self.off = 0

    def alloc(self, name, shape, dt):
        per = 1
        for s in shape[1:]:
            per *= s
        nb = per * (4 if dt == F32 else 2)
        nb = (nb + 63) // 64 * 64
        ne = nb // 2
        assert self.off + ne <= ARENA, (name, shape, self.off)
        v = self.arena[0:shape[0], self.off:self.off + per * (2 if dt == F32 else 1)]
        self.off += ne
        if dt == F32:
            v = v.bitcast(F32)
        if len(shape) == 3:
            v = v.rearrange("p (a b) -> p a b", b=shape[2])
        return v, Res(name)

    def barrier(self):
        deps = []
        for e in CENGS:
            for op in reversed(self.ops[e]):
                if not op.dma:
                    deps.append(op)
                    break
        deps.extend(self.stage_dmas)
        self.stage_dmas = []
        for e in ENGS:
            prev = self.pending[e] or []
            self.pending[e] = prev + deps
        self.stage += 1
        self.off = 0
        self.stage_keys = {}

    def _add(self, eng, fn, reads, writes, dma=False, key=None, chain=False):
        grp = self.stage % NGRP
        if key is not None:
            if key not in self.stage_keys:
                self.stage_keys[key] = len(self.stage_keys)
            key = f"{grp}_{self.stage_keys[key]}"
        op = Op(eng, fn, dma, key, grp)
        deps = []
        for r in reads:
            if r.last_w is not None:
                deps.append(r.last_w)
        for w in writes:
            if w.last_w is not None:
                if not (chain and w.last_w.dma and w.last_w.key == key):
                    deps.append(w.last_w)
                else:
                    deps.extend(w.last_w.deps)
            deps.extend(w.readers)
        if self.pending[eng]:
            deps.extend(self.pending[eng])
            self.pending[eng] = None
        seen = set()
        for d in deps:
            if id(d) not in seen and d is not op:
                seen.add(id(d))
                op.deps.append(d)
        for r in reads:
            r.readers.append(op)
        for w in writes:
            w.last_w = op
            w.readers = []
        if dma:
            if key not in self.dma_sems:
                self.dma_sems[key] = self.stack.enter_context(self.nc.semaphore("ds_" + key))
                self.dma_cnt[key] = 0
            self.dma_cnt[key] += 16
            op.sem = self.dma_sems[key]
            op.val = self.dma_cnt[key]
            self.stage_dmas.append(op)
        self.ops[eng].append(op)
        return op

    def op(self, eng, fn, reads=(), writes=()):
        return self._add(eng, fn, reads, writes)

    def dma(self, eng, key, out, in_, reads=(), writes=(), chain=False, is_out=False):
        op = self._add(eng, lambda e, out=out, in_=in_: e.dma_start(out=out, in_=in_), reads, writes,
                       dma=True, key=key, chain=chain)
        if is_out:
            self.out_dmas.append(op)
        return op

    def emit(self):
        fin = Op("sp", None, False, None, 0)
        fin.deps = list(self.out_dmas)
        self.ops["sp"].append(fin)
        for e in ENGS:
            for op in self.ops[e]:
                for d in op.deps:
                    if not d.dma:
                        if d.eng == "pe" and op.eng == "pe" and not op.dma:
                            continue
                        d.signal = True
        for e in CENGS:
            cnt = [0] * NGRP
            for op in self.ops[e]:
                if not op.dma and op.signal:
                    cnt[op.grp] += 1
                    op.sem = self.eng_sem[(e, op.grp)]
                    op.val = cnt[op.grp]
        nc = self.nc
        ops = self.ops

        def run(e, eng):
            seen = {}
            for op in ops[e]:
                need = {}
                for d in op.deps:
                    if not d.dma and d.eng == "pe" and e == "pe" and not op.dma:
                        continue
                    k = id(d.sem)
                    if seen.get(k, 0) >= d.val:
                        continue
                    if k not in need or need[k][1] < d.val:
                        need[k] = (d.sem, d.val)
                for k, (s, v) in need.items():
                    eng.wait_ge(s, v)
                    seen[k] = v
                if op.fn is None:
                    continue
                ins = op.fn(eng)
                if op.signal:
                    ins.then_inc(op.sem, 16 if op.dma else 1)

        with nc.Block() as block:
            @block.tensor
            def _(eng):
                run("pe", eng)

            @block.scalar
            def _(eng):
                run("act", eng)

            @block.vector
            def _(eng):
                run("dve", eng)

            @block.gpsimd
            def _(eng):
                run("pool", eng)

            @block.sync
            def _(eng):
                run("sp", eng)


def _nblocks(n, nb=512):
    return [(i, min(nb, n - i)) for i in range(0, n, nb)]


class PS:
    def __init__(self, nc, st):
        self.f = [st.enter_context(nc.psum_tensor(f"psf{i}", [128, 512], F32)) for i in range(5)]
        self.fr = [Res(f"psf{i}") for i in range(5)]
        self.t = [st.enter_context(nc.psum_tensor(f"pst{i}", [128, 4, 128], F32)) for i in range(3)]
        self.tr = [Res(f"pst{i}") for i in range(3)]


def st_gemm(p, ps, at, MT, KC, bs, N, epi, out, r_ap=None, grow=None, brow=None):
    odt = out.dtype
    nB = len(bs)
    NB = 512
    tiled = [len(b.shape) == 4 for b in bs]
    bv = [None if tiled[i] else b.rearrange("(kc p) n -> p kc n", p=128) for i, b in enumerate(bs)]
    RES_AT = MT * KC * 256 <= 40 * 1024
    NAT = MT if RES_AT else 3
    at_bytes = NAT * KC * 256
    nbuf = 2 if (KC * NB * 2 * nB * 2 + at_bytes + 24 * 1024 <= ARENA * 2) else 1
    bsb = [[p.alloc(f"bsb{i}_{j}", [128, KC, NB], BF16) for i in range(nB)] for j in range(nbuf)]
    att = [p.alloc(f"at{i}", [128, KC, 128], BF16) for i in range(NAT)]
    if RES_AT:
        for mt in range(MT):
            p.dma("sp", "atres", att[mt][0][:, :, :], at[mt], writes=[att[mt][1]])
    ot = [p.alloc(f"ot{i}", [128, NB], odt) for i in range(2)]
    if epi != "plain":
        tmp = [p.alloc(f"tmp{i}", [128, NB], F32) for i in range(2)]
    if epi == "resid":
        rt = [p.alloc(f"rt{i}", [128, NB], F32) for i in range(2)]
        gt = [p.alloc(f"gt{i}", [128, NB], F32) for i in range(2)]
    if epi == "bias":
        bt = [p.alloc(f"bt{i}", [128, NB], F32) for i in range(2)]
    NPS = 4 // nB
    it = 0
    KG = 8
    for bi, (n0, nb) in enumerate(_nblocks(N, NB)):
        bb = bsb[bi % nbuf]
        for i in range(nB):
            for k0 in range(0, KC, KG):
                k1 = min(KC, k0 + KG)
                if tiled[i]:
                    p.dma("pool", f"b{i}_{bi % nbuf}", bb[i][0][:, k0:k1, :], bs[i][bi][:, k0:k1, :],
                          writes=[bb[i][1]], chain=(k0 > 0))
                else:
                    p.dma("pool", f"b{i}_{bi % nbuf}", bb[i][0][:, k0:k1, 0:nb], bv[i][:, k0:k1, n0:n0 + nb],
                          writes=[bb[i][1]], chain=(k0 > 0))
        if epi == "bias":
            p.dma("pool", f"bt{bi % 2}", bt[bi % 2][0][:, 0:nb], brow[0:1, n0:n0 + nb].partition_broadcast(128), writes=[bt[bi % 2][1]])
        for mt in range(MT):
            s = it % 2
            a = it % NAT
            q = it % NPS
            it += 1
            if RES_AT:
                a = mt
            else:
                p.dma("sp", f"at{a}", att[a][0][:, :, :], at[mt], writes=[att[a][1]])
            if epi == "resid":
                p.dma("act", f"rt{s}", rt[s][0][:, 0:nb], r_ap[mt * 128:(mt + 1) * 128, n0:n0 + nb], writes=[rt[s][1]])
                p.dma("act", f"gt{s}", gt[s][0][:, 0:nb], grow(mt)[0:1, n0:n0 + nb].partition_broadcast(128), writes=[gt[s][1]])
            for i in range(nB):
                z = q * nB + i

                def mm(e, i=i, a=a, z=z, nb=nb, bb=bb):
                    ins = None
                    for kc in range(KC):
                        ins = e.matmul(ps.f[z][:, 0:nb], att[a][0][:, kc, :], bb[i][0][:, kc, 0:nb],
                                       start=(kc == 0), stop=(kc == KC - 1))
                    return ins
                p.op("pe", mm, reads=[att[a][1], bb[i][1]], writes=[ps.fr[z]])
            z0 = q * nB
            if epi == "plain":
                if it % 2 == 0:
                    p.op("act", lambda e, s=s, z0=z0, nb=nb: e.activation(out=ot[s][0][:, 0:nb], in_=ps.f[z0][:, 0:nb], func=AF.Copy),
                         reads=[ps.fr[z0]], writes=[ot[s][1]])
                else:
                    p.op("dve", lambda e, s=s, z0=z0, nb=nb: e.tensor_copy(out=ot[s][0][:, 0:nb], in_=ps.f[z0][:, 0:nb]),
                         reads=[ps.fr[z0]], writes=[ot[s][1]])
            elif epi == "silu":
                p.op("act", lambda e, s=s, z0=z0, nb=nb: e.activation(out=tmp[s][0][:, 0:nb], in_=ps.f[z0][:, 0:nb], func=AF.Silu),
                     reads=[ps.fr[z0]], writes=[tmp[s][1]])
                p.op("dve", lambda e, s=s, z0=z0, nb=nb: e.tensor_tensor(out=ot[s][0][:, 0:nb], in0=tmp[s][0][:, 0:nb], in1=ps.f[z0 + 1][:, 0:nb], op=ALU.mult),
                     reads=[tmp[s][1], ps.fr[z0 + 1]], writes=[ot[s][1]])
            elif epi == "resid":
                p.op("dve", lambda e, s=s, z0=z0, nb=nb: e.tensor_tensor(out=tmp[s][0][:, 0:nb], in0=ps.f[z0][:, 0:nb], in1=gt[s][0][:, 0:nb], op=ALU.mult),
                     reads=[ps.fr[z0], gt[s][1]], writes=[tmp[s][1]])
                p.op("pool", lambda e, s=s, nb=nb: e.tensor_tensor(out=ot[s][0][:, 0:nb], in0=tmp[s][0][:, 0:nb], in1=rt[s][0][:, 0:nb], op=ALU.add),
                     reads=[tmp[s][1], rt[s][1]], writes=[ot[s][1]])
            else:
                p.op("dve", lambda e, s=s, z0=z0, nb=nb, bi=bi: e.tensor_tensor(out=ot[s][0][:, 0:nb], in0=ps.f[z0][:, 0:nb], in1=bt[bi % 2][0][:, 0:nb], op=ALU.add),
                     reads=[ps.fr[z0], bt[bi % 2][1]], writes=[ot[s][1]])
            p.dma("sp" if RES_AT else STQ, f"ot{s}", out[mt * 128:(mt + 1) * 128, n0:n0 + nb], ot[s][0][:, 0:nb], reads=[ot[s][1]])
    p.barrier()


def st_trans(p, ps, ident, src_fn, MT, K, at_out):
    KC = K // 128
    xt = [p.alloc(f"xt{i}", [128, K], BF16) for i in range(2)]
    att = [p.alloc(f"att{i}", [128, KC, 128], BF16) for i in range(2)]
    iT = 0
    for mt in range(MT):
        s = mt % 2
        for j, (c0, ap) in enumerate(src_fn(mt)):
            w = ap.shape[1]
            p.dma("sp" if j % 2 == 0 else "act", f"xt{s}", xt[s][0][:, c0:c0 + w], ap, writes=[xt[s][1]], chain=(j > 0))
        for c0 in range(0, KC, 4):
            c1 = min(KC, c0 + 4)
            z = iT % 3
            iT += 1

            def tr(e, s=s, z=z, c0=c0, c1=c1):
                ins = None
                for c in range(c0, c1):
                    ins = e.matmul(ps.t[z][:, c - c0, :], xt[s][0][:, c * 128:(c + 1) * 128], ident[0][:, :], start=True, stop=True)
                return ins
            p.op("pe", tr, reads=[xt[s][1], ident[1]], writes=[ps.tr[z]])
            if (c0 // 4) % 2 == 0:
                p.op("dve", lambda e, s=s, z=z, c0=c0, c1=c1: e.tensor_copy(out=att[s][0][:, c0:c1, :], in_=ps.t[z][:, 0:c1 - c0, :]),
                     reads=[ps.tr[z]], writes=[att[s][1]])
            else:
                p.op("act", lambda e, s=s, z=z, c0=c0, c1=c1: e.activation(out=att[s][0][:, c0:c1, :], in_=ps.t[z][:, 0:c1 - c0, :], func=AF.Copy),
                     reads=[ps.tr[z]], writes=[att[s][1]])
        p.dma("pool", f"att{s}", at_out[mt], att[s][0][:, :, :], reads=[att[s][1]])
    p.barrier()


def st_norm(p, x_fn, NT, D, g_row, sc_fn, sh_fn, out_fn, out_dt, final=False):
    gb = p.alloc("gb", [128, D], F32)
    xt = [p.alloc(f"xt{i}", [128, D], F32) for i in range(2)]
    sct = [p.alloc(f"sct{i}", [128, D], F32) for i in range(2)]
    sht = [p.alloc(f"sht{i}", [128, D], F32) for i in range(2)]
    sq = p.alloc("sq", [128, D], F32)
    ss = [p.alloc(f"ss{i}", [128, 4], F32) for i in range(2)]
    ot = [p.alloc(f"ot{i}", [128, D], out_dt) for i in range(2)]
    p.dma("pool", "gb", gb[0][:, :], g_row.partition_broadcast(128), writes=[gb[1]])
    for t in range(NT):
        s = t % 2
        p.dma("sp", f"xt{s}", xt[s][0][:, :], x_fn(t), writes=[xt[s][1]])
        p.dma("pool", f"sct{s}", sct[s][0][:, :], sc_fn(t).partition_broadcast(128), writes=[sct[s][1]])
        p.dma("pool", f"sht{s}", sht[s][0][:, :], sh_fn(t).partition_broadcast(128), writes=[sht[s][1]])
        p.op("act", lambda e, s=s: e.activation(out=sq[0][:, :], in_=xt[s][0][:, :], func=AF.Square, accum_out=ss[s][0][:, 0:1]),
             reads=[xt[s][1]], writes=[sq[1], ss[s][1]])
        p.op("dve", lambda e, s=s: e.tensor_scalar(out=ss[s][0][:, 1:2], in0=ss[s][0][:, 0:1], scalar1=1.0 / D, scalar2=EPS, op0=ALU.mult, op1=ALU.add),
             reads=[ss[s][1]], writes=[ss[s][1]])
        p.op("act", lambda e, s=s: e.activation(out=ss[s][0][:, 2:3], in_=ss[s][0][:, 1:2], func=AF.Sqrt),
             reads=[ss[s][1]], writes=[ss[s][1]])
        p.op("dve", lambda e, s=s: e.reciprocal(out=ss[s][0][:, 3:4], in_=ss[s][0][:, 2:3]),
             reads=[ss[s][1]], writes=[ss[s][1]])
        p.op("dve", lambda e, s=s: e.scalar_tensor_tensor(out=sct[s][0][:, :], in0=sct[s][0][:, :], scalar=1.0, in1=gb[0][:, :], op0=ALU.add, op1=ALU.mult),
             reads=[sct[s][1], gb[1]], writes=[sct[s][1]])
        p.op("dve", lambda e, s=s: e.scalar_tensor_tensor(out=xt[s][0][:, :], in0=xt[s][0][:, :], scalar=ss[s][0][:, 3:4], in1=sct[s][0][:, :], op0=ALU.mult, op1=ALU.mult),
             reads=[xt[s][1], ss[s][1], sct[s][1]], writes=[xt[s][1]])
        p.op("pool", lambda e, s=s: e.tensor_tensor(out=ot[s][0][:, :], in0=xt[s][0][:, :], in1=sht[s][0][:, :], op=ALU.add),
             reads=[xt[s][1], sht[s][1]], writes=[ot[s][1]])
        p.dma("pool", f"ot{s}", out_fn(t), ot[s][0][:, :], reads=[ot[s][1]], is_out=final)
    p.barrier()


def st_silu(p, a_ap, out_ap, Fd):
    at = p.alloc("sa", [128, Fd], F32)
    ot = p.alloc("so", [128, Fd], BF16)
    p.dma("sp", "sa", at[0][:, :], a_ap, writes=[at[1]])
    p.op("act", lambda e: e.activation(out=ot[0][:, :], in_=at[0][:, :], func=AF.Silu), reads=[at[1]], writes=[ot[1]])
    p.dma("sp", "so", out_ap, ot[0][:, :], reads=[ot[1]])
    p.barrier()


def st_muladd(p, NT, shape, a_dt, a_fn, b_fn, c_fn, d_fn, out_fn):
    dts = [a_dt, F32, a_dt, F32]
    fns = [a_fn, b_fn, c_fn, d_fn]
    names = "abcd"
    tl = [[p.alloc(f"{names[j]}{i}", shape, dts[j]) for i in range(2)] for j in range(4)]
    t1 = [p.alloc(f"t1{i}", shape, F32) for i in range(2)]
    t2 = [p.alloc(f"t2{i}", shape, F32) for i in range(2)]
    ot = [p.alloc(f"ot{i}", shape, BF16) for i in range(2)]
    for t in range(NT):
        s = t % 2
        for j in range(4):
            p.dma("sp" if j % 2 == 0 else "pool", f"{names[j]}{s}", tl[j][s][0], fns[j](t), writes=[tl[j][s][1]])
        p.op("dve", lambda e, s=s: e.tensor_tensor(out=t1[s][0], in0=tl[0][s][0], in1=tl[1][s][0], op=ALU.mult),
             reads=[tl[0][s][1], tl[1][s][1]], writes=[t1[s][1]])
        p.op("pool", lambda e, s=s: e.tensor_tensor(out=t2[s][0], in0=tl[2][s][0], in1=tl[3][s][0], op=ALU.mult),
             reads=[tl[2][s][1], tl[3][s][1]], writes=[t2[s][1]])
        p.op("dve", lambda e, s=s: e.tensor_tensor(out=ot[s][0], in0=t1[s][0], in1=t2[s][0], op=ALU.add),
             reads=[t1[s][1], t2[s][1]], writes=[ot[s][1]])
        p.dma("act", f"ot{s}", out_fn(t), ot[s][0], reads=[ot[s][1]])
    p.barrier()


def st_attn(p, ps, ident, H, NQ, NK, qx, qrot, kvx, krot, o):
    NKC = NK // 128
    NQT = NQ // 128
    kbl = _nblocks(NK, 512)
    krtok = p.alloc("krtok", [128, NKC, ROPE], BF16)
    krT = p.alloc("krT", [64, NK], BF16)
    qtok = [p.alloc(f"qtok{i}", [128, NQT, NOPE], BF16) for i in range(2)]
    qrtok = [p.alloc(f"qrtok{i}", [128, NQT, ROPE], BF16) for i in range(2)]
    ktok = [p.alloc(f"ktok{i}", [128, NKC, NOPE], BF16) for i in range(2)]
    vt = [p.alloc(f"vt{i}", [128, NKC, VH], BF16) for i in range(2)]
    qn = [p.alloc(f"qn{i}", [128, NQ], BF16) for i in range(2)]
    qr_ = [p.alloc(f"qr{i}", [64, NQ], BF16) for i in range(2)]
    kn = [p.alloc(f"kn{i}", [128, NK], BF16) for i in range(2)]
    S = [p.alloc(f"S{i}", [128, NK], F32) for i in range(2)]
    P = [p.alloc(f"P{i}", [128, NK], BF16) for i in range(2)]
    PT = [p.alloc(f"PT{i}", [128, NKC, 128], BF16) for i in range(2)]
    st4 = [p.alloc(f"st{i}", [128, 4], F32) for i in range(2)]
    ot = [p.alloc(f"ot{i}", [128, 128], BF16) for i in range(2)]
    cnt = {"S": 0, "T": 0, "q": 0, "e": 0}

    def trans_into(src, dst, nchunk, rows_out):
        for c0 in range(0, nchunk, 4):
            c1 = min(nchunk, c0 + 4)
            z = cnt["T"] % 3
            cnt["T"] += 1

            def tr(e, z=z, c0=c0, c1=c1):
                ins = None
                for c in range(c0, c1):
                    ins = e.matmul(ps.t[z][0:rows_out, c - c0, :], src[0][:, c, :], ident[0][:, :], start=True, stop=True)
                return ins
            p.op("pe", tr, reads=[src[1], ident[1]], writes=[ps.tr[z]])
            cnt["e"] += 1
            dv = dst[0][:, c0 * 128:c1 * 128].rearrange("p (a b) -> p a b", b=128)
            if cnt["e"] % 2 == 0:
                p.op("dve", lambda e, z=z, c0=c0, c1=c1, dv=dv: e.tensor_copy(out=dv, in_=ps.t[z][0:rows_out, 0:c1 - c0, :]),
                     reads=[ps.tr[z]], writes=[dst[1]])
            else:
                p.op("act", lambda e, z=z, c0=c0, c1=c1, dv=dv: e.activation(out=dv, in_=ps.t[z][0:rows_out, 0:c1 - c0, :], func=AF.Copy),
                     reads=[ps.tr[z]], writes=[dst[1]])

    p.dma("sp", "krtok", krtok[0][:, :, :], krot.rearrange("(c p) d -> p c d", p=128), writes=[krtok[1]])
    trans_into(krtok, krT, NKC, ROPE)
    def prologue(h):
        b = h % 2
        p.dma("sp", f"qtok{b}", qtok[b][0][:, :, :], qx[:, h * QKH:h * QKH + NOPE].rearrange("(c p) d -> p c d", p=128), writes=[qtok[b][1]])
        p.dma("sp", f"qrtok{b}", qrtok[b][0][:, :, :], qrot[:, h * ROPE:(h + 1) * ROPE].rearrange("(c p) d -> p c d", p=128), writes=[qrtok[b][1]])
        p.dma("sp", f"ktok{b}", ktok[b][0][:, :, :], kvx[:, h * 256:h * 256 + NOPE].rearrange("(c p) d -> p c d", p=128), writes=[ktok[b][1]])
        p.dma("act", f"vt{b}", vt[b][0][:, :, :], kvx[:, h * 256 + NOPE:(h + 1) * 256].rearrange("(c p) d -> p c d", p=128), writes=[vt[b][1]])
        trans_into(ktok[b], kn[b], NKC, 128)
        trans_into(qtok[b], qn[b], NQT, 128)
        trans_into(qrtok[b], qr_[b], NQT, ROPE)

    def phase_a(idx, h, qt):
        b = h % 2
        s = idx % 2
        q0 = qt * 128
        for bi, (k0, kb) in enumerate(kbl):
            z = cnt["S"] % 3
            cnt["S"] += 1

            def mmS(e, b=b, z=z, q0=q0, k0=k0, kb=kb):
                e.matmul(ps.f[z][:, 0:kb], qn[b][0][:, q0:q0 + 128], kn[b][0][:, k0:k0 + kb], start=True, stop=False)
                return e.matmul(ps.f[z][:, 0:kb], qr_[b][0][:, q0:q0 + 128], krT[0][:, k0:k0 + kb], start=False, stop=True)
            p.op("pe", mmS, reads=[qn[b][1], qr_[b][1], kn[b][1], krT[1]], writes=[ps.fr[z]])
            if bi % 2 == 1:
                p.op("dve", lambda e, s=s, z=z, k0=k0, kb=kb: e.tensor_copy(out=S[s][0][:, k0:k0 + kb], in_=ps.f[z][:, 0:kb]),
                     reads=[ps.fr[z]], writes=[S[s][1]])
            else:
                p.op("act", lambda e, s=s, z=z, k0=k0, kb=kb: e.activation(out=S[s][0][:, k0:k0 + kb], in_=ps.f[z][:, 0:kb], func=AF.Copy),
                     reads=[ps.fr[z]], writes=[S[s][1]])
        p.op("dve", lambda e, s=s: e.reduce_max(out=st4[s][0][:, 0:1], in_=S[s][0][:, :], axis=AX.X),
             reads=[S[s][1]], writes=[st4[s][1]])
        p.op("dve", lambda e, s=s: e.tensor_scalar(out=st4[s][0][:, 1:2], in0=st4[s][0][:, 0:1], scalar1=-SCALE, scalar2=None, op0=ALU.mult),
             reads=[st4[s][1]], writes=[st4[s][1]])
        p.op("act", lambda e, s=s: e.activation(out=P[s][0][:, :], in_=S[s][0][:, :], func=AF.Exp, bias=st4[s][0][:, 1:2], scale=SCALE, accum_out=st4[s][0][:, 2:3]),
             reads=[S[s][1], st4[s][1]], writes=[P[s][1], st4[s][1]])

    def phase_b(idx, h, qt):
        b = h % 2
        s = idx % 2
        q0 = qt * 128
        Pv = (P[s][0].rearrange("p (a b) -> p a b", b=128), P[s][1])
        PTv = (PT[s][0].rearrange("p a b -> p (a b)"), PT[s][1])
        trans_into(Pv, PTv, NKC, 128)
        zo = 3 + (idx % 2)

        def mmO(e, s=s, b=b, zo=zo):
            ins = None
            for c in range(NKC):
                ins = e.matmul(ps.f[zo][:, 0:128], PT[s][0][:, c, :], vt[b][0][:, c, :], start=(c == 0), stop=(c == NKC - 1))
            return ins
        p.op("pe", mmO, reads=[PT[s][1], vt[b][1]], writes=[ps.fr[zo]])
        p.op("dve", lambda e, s=s: e.reciprocal(out=st4[s][0][:, 3:4], in_=st4[s][0][:, 2:3]),
             reads=[st4[s][1]], writes=[st4[s][1]])
        p.op("act", lambda e, s=s, zo=zo: e.activation(out=ot[s][0][:, :], in_=ps.f[zo][:, 0:128], func=AF.Copy, scale=st4[s][0][:, 3:4]),
             reads=[ps.fr[zo], st4[s][1]], writes=[ot[s][1]])
        p.dma("pool", f"ot{s}", o[q0:q0 + 128, h * VH:(h + 1) * VH], ot[s][0][:, :], reads=[ot[s][1]])

    items = [(h, qt) for h in range(H) for qt in range(NQT)]
    for idx, (h, qt) in enumerate(items):
        if idx == 0:
            prologue(h)
            phase_a(0, h, qt)
        if idx + 1 < len(items):
            h1, qt1 = items[idx + 1]
            if qt1 == 0:
                prologue(h1)
            phase_a(idx + 1, h1, qt1)
        phase_b(idx, h, qt)
    p.barrier()


def build_fused(cfg, debug=False):
    D, SEQ, CTX, H, QL, KVL, FF = cfg["D"], cfg["SEQ"], cfg["CTX"], cfg["HEADS"], cfg["QL"], cfg["KVL"], cfg["FFN"]
    T = SEQ + CTX
    TQ = SEQ // 4
    MTt, MTq = T // 128, TQ // 128
    NLt = SEQ // 128
    GD = D // 4
    DC = D // 128
    NWA = QL + KVL + 2 * ROPE
    NQX = H * QKH + H * ROPE
    nc = bass.Bass("TRN2", target_bir_lowering=False)
    I = lambda name, shape, dt: nc.dram_tensor(name, shape, dt, kind="ExternalInput").ap()
    W = lambda name, shape, dt=BF16: nc.dram_tensor(name, shape, dt, kind=("ExternalOutput" if debug else "Internal")).ap()
    xin = I("xin", [T, D], F32)
    cv = I("cv", [128, D], F32)
    w_ada = [I(f"w_ada{i}", [-(-(6 * D) // 512), 128, (D) // 128, 512], F32) for i in range(2)]
    b_ada = [I(f"b_ada{i}", [1, 6 * D], F32) for i in range(2)]
    g_mix = [I(f"g_mix{i}", [1, D], F32) for i in range(2)]
    g_ffn = [I(f"g_ffn{i}", [1, D], F32) for i in range(2)]
    g_fin = I("g_fin", [1, D], F32)
    g_q = I("g_q", [1, QL], F32)
    g_kv = I("g_kv", [1, KVL], F32)
    zrow = I("zrow", [1, D], F32)
    w_out = I("w_out", [-(-(D) // 512), 128, (D) // 128, 512], F32)
    w_a = I("w_a", [-(-(NWA) // 512), 128, (D) // 128, 512], F32)
    w_uq = I("w_uq", [-(-(NQX) // 512), 128, (QL) // 128, 512], F32)
    w_ukv = I("w_ukv", [-(-(H * 256) // 512), 128, (KVL) // 128, 512], F32)
    w_o = I("w_o", [-(-(D) // 512), 128, (H * VH) // 128, 512], F32)
    w_gate = [I(f"w_gate{i}", [-(-(FF) // 512), 128, (D) // 128, 512], F32) for i in range(2)]
    w_up = [I(f"w_up{i}", [-(-(FF) // 512), 128, (D) // 128, 512], F32) for i in range(2)]
    w_down = [I(f"w_down{i}", [-(-(D) // 512), 128, (FF) // 128, 512], F32) for i in range(2)]
    apos = I("apos", [2 * T // 128, 128, MTt, 128], BF16)
    bch = I("bch", [2 * GD, GD], BF16)
    ident_in = I("ident", [128, 128], BF16)
    cosq = I("cosq", [TQ, H * ROPE], F32)
    sinq = I("sinq", [TQ, H * ROPE], F32)
    cosk = I("cosk", [T, ROPE], F32)
    sink = I("sink", [T, ROPE], F32)
    out = nc.dram_tensor("out", [TQ, D], F32, kind="ExternalOutput").ap()
    s_act = W("s_act", [128, D]); s_at = W("s_at", [1, 128, DC, 128])
    mod = [W(f"mod{i}", [128, 6 * D], F32) for i in range(2)]
    Hh = W("Hh", [T, D])
    Pp = W("Pp", [2 * T, D])
    a2_at = W("a2_at", [MTt, 128, 2 * GD // 128, 128])
    Fa = W("Fa", [T, D]); F_at = W("F_at", [MTt, 128, DC, 128])
    X1 = W("X1", [T, D], F32); X2 = W("X2", [T, D], F32)
    H_at = W("H_at", [MTt, 128, DC, 128])
    aa = W("aa", [T, FF]); a_at = W("a_at", [MTt, 128, FF // 128, 128])
    ca = W("ca", [T, NWA], F32)
    cqn = W("cqn", [TQ, QL]); cqn_at = W("cqn_at", [MTq, 128, QL // 128, 128])
    ckvn = W("ckvn", [T, KVL]); ckvn_at = W("ckvn_at", [MTt, 128, KVL // 128, 128])
    qx = W("qx", [TQ, NQX]); kvx = W("kvx", [T, H * 256])
    qrot = W("qrot", [TQ, H * ROPE]); krot = W("krot", [T, ROPE])
    oo = W("oo", [TQ, H * VH]); o_at = W("o_at", [MTq, 128, H * VH // 128, 128])
    X3 = W("X3", [TQ, D], F32); X4 = W("X4", [TQ, D], F32)

    with ExitStack() as st:
        p = Prog(nc, st)
        ps = PS(nc, st)
        idt = st.enter_context(nc.sbuf_tensor("ident_sb", [128, 128], BF16))
        ident = (idt, Res("ident"))
        p.dma("sp", "ident", idt[:, :], ident_in[:, :], writes=[ident[1]])
        grp = lambda mt: 0 if mt < NLt else 1

        def modrow(i, j):
            return lambda mt: mod[i][grp(mt):grp(mt) + 1, j * D:(j + 1) * D]

        def rows(ap, c0=0, c1=None):
            return lambda mt: [(0, ap[mt * 128:(mt + 1) * 128, c0:(c1 if c1 is not None else ap.shape[1])])]

        st_silu(p, cv[:, :], s_act[:, :], D)
        st_trans(p, ps, ident, rows(s_act), 1, D, s_at)
        for i in range(2):
            st_gemm(p, ps, s_at, 1, DC, [w_ada[i]], 6 * D, "bias", mod[i], brow=b_ada[i])

        def norm_mod(Xs, nt, i, jsh, jsc, g_row, out_ap):
            st_norm(p, lambda t: Xs[t * 128:(t + 1) * 128, :], nt, D, g_row,
                    lambda t: modrow(i, jsc)(t), lambda t: modrow(i, jsh)(t),
                    lambda t: out_ap[t * 128:(t + 1) * 128, :], BF16)

        def ffn(Xs, Xd, nt, i):
            norm_mod(Xs, nt, i, 3, 4, g_ffn[i][0:1, :], Hh)
            st_trans(p, ps, ident, rows(Hh), nt, D, H_at)
            st_gemm(p, ps, H_at, nt, DC, [w_gate[i], w_up[i]], FF, "silu", aa)
            st_trans(p, ps, ident, rows(aa), nt, FF, a_at)
            st_gemm(p, ps, a_at, nt, FF // 128, [w_down[i]], D, "resid", Xd, r_ap=Xs, grow=modrow(i, 5))

        norm_mod(xin, MTt, 0, 0, 1, g_mix[0][0:1, :], Hh)
        st_gemm(p, ps, apos, 2 * MTt, MTt, [Hh], D, "plain", Pp)
        for g in range(4):
            def a2src(mt, g=g):
                if mt < NLt:
                    r0, r1 = mt * 128, SEQ + mt * 128
                else:
                    r0, r1 = 2 * SEQ + (mt - NLt) * 128, 2 * SEQ + CTX + (mt - NLt) * 128
                return [(0, Pp[r0:r0 + 128, g * GD:(g + 1) * GD]), (GD, Pp[r1:r1 + 128, g * GD:(g + 1) * GD])]
            st_trans(p, ps, ident, a2src, MTt, 2 * GD, a2_at)
            st_gemm(p, ps, a2_at, MTt, 2 * GD // 128, [bch], GD, "plain", Fa[:, g * GD:(g + 1) * GD])
        st_trans(p, ps, ident, rows(Fa), MTt, D, F_at)
        st_gemm(p, ps, F_at, MTt, DC, [w_out], D, "resid", X1, r_ap=xin, grow=modrow(0, 2))
        ffn(X1, X2, MTt, 0)

        norm_mod(X2, MTt, 1, 0, 1, g_mix[1][0:1, :], Hh)
        st_trans(p, ps, ident, rows(Hh), MTt, D, H_at)
        st_gemm(p, ps, H_at, MTt, DC, [w_a], NWA, "plain", ca)
        zr = lambda n: (lambda t: zrow[0:1, 0:n])
        st_norm(p, lambda t: ca[t * 128:(t + 1) * 128, 0:QL], MTq, QL, g_q[0:1, :], zr(QL), zr(QL),
                lambda t: cqn[t * 128:(t + 1) * 128, :], BF16)
        st_norm(p, lambda t: ca[t * 128:(t + 1) * 128, QL:QL + KVL], MTt, KVL, g_kv[0:1, :], zr(KVL), zr(KVL),
                lambda t: ckvn[t * 128:(t + 1) * 128, :], BF16)
        st_trans(p, ps, ident, rows(cqn), MTq, QL, cqn_at)
        st_trans(p, ps, ident, rows(ckvn), MTt, KVL, ckvn_at)
        st_gemm(p, ps, cqn_at, MTq, QL // 128, [w_uq], NQX, "plain", qx)
        st_gemm(p, ps, ckvn_at, MTt, KVL // 128, [w_ukv], H * 256, "plain", kvx)
        rs = lambda t: slice(t * 128, (t + 1) * 128)
        h3 = lambda ap: ap.rearrange("t (h d) -> t h d", d=ROPE)
        st_muladd(p, MTq, [128, H, ROPE], BF16,
                  lambda t: qx[rs(t), 0:H * QKH].rearrange("t (h d) -> t h d", d=QKH)[:, :, NOPE:QKH],
                  lambda t: h3(cosq[rs(t), :]),
                  lambda t: h3(qx[rs(t), H * QKH:NQX]),
                  lambda t: h3(sinq[rs(t), :]),
                  lambda t: h3(qrot[rs(t), :]))
        st_muladd(p, MTt, [128, ROPE], F32,
                  lambda t: ca[rs(t), QL + KVL:QL + KVL + ROPE], lambda t: cosk[rs(t), :],
                  lambda t: ca[rs(t), QL + KVL + ROPE:NWA], lambda t: sink[rs(t), :],
                  lambda t: krot[rs(t), :])
        st_attn(p, ps, ident, H, TQ, T, qx, qrot, kvx, krot, oo)
        st_trans(p, ps, ident, rows(oo), MTq, H * VH, o_at)
        st_gemm(p, ps, o_at, MTq, H * VH // 128, [w_o], D, "resid", X3, r_ap=X2, grow=modrow(1, 2))
        ffn(X3, X4, MTq, 1)
        st_norm(p, lambda t: X4[t * 128:(t + 1) * 128, :], MTq, D, g_fin[0:1, :], zr(D), zr(D),
                lambda t: out[t * 128:(t + 1) * 128, :], F32, final=True)
        p.emit()
    return nc


def tile_at(A):
    M, K = A.shape
    return np.ascontiguousarray(A.reshape(M // 128, 128, K // 128, 128).transpose(0, 3, 2, 1))


def tile_b(W):
    K, N = W.shape
    nblk = -(-N // 512)
    if nblk * 512 != N:
        W = np.concatenate([W, np.zeros((K, nblk * 512 - N), W.dtype)], 1)
    return np.ascontiguousarray(W.reshape(K // 128, 128, nblk, 512).transpose(2, 1, 0, 3))


def _swap_idx():
    q = ROPE // 4
    return np.concatenate([np.arange(q, 2 * q), np.arange(0, q), np.arange(3 * q, 4 * q), np.arange(2 * q, 3 * q)])


def host_inputs(cfg, x, c, ctx, c_ctx, w_ada, b_ada, g_mix, g_ffn, fourier_w_out, mla_w_a, mla_g_q, mla_g_kv,
                mla_w_uq, mla_w_ukv, mla_w_o, w_gate, w_up, w_down, g_final):
    f32 = np.float32
    D, SEQ, CTX, H, QL, KVL, FF, GW = cfg["D"], cfg["SEQ"], cfg["CTX"], cfg["HEADS"], cfg["QL"], cfg["KVL"], cfg["FFN"], cfg["GRID_W"]
    T, TQ, GD = SEQ + CTX, SEQ // 4, D // 4
    A = lambda z: np.ascontiguousarray(np.asarray(z, f32))
    row = lambda z: A(z).reshape(1, -1)
    sw = _swap_idx()
    wa = A(mla_w_a[0])
    w_a_ext = np.ascontiguousarray(np.concatenate([wa, wa[:, QL + KVL + sw]], 1))
    wq = A(mla_w_uq[0])
    wq3 = wq.reshape(QL, H, QKH)
    w_uq_ext = np.ascontiguousarray(np.concatenate([wq, wq3[:, :, NOPE + sw].reshape(QL, H * ROPE)], 1))
    shared = {
        "g_fin": row(g_final), "g_q": row(mla_g_q[0]), "g_kv": row(mla_g_kv[0]), "zrow": np.zeros((1, D), f32),
        "w_out": tile_b(A(fourier_w_out[0])), "w_a": tile_b(w_a_ext), "w_uq": tile_b(w_uq_ext), "w_ukv": tile_b(A(mla_w_ukv[0])), "w_o": tile_b(A(mla_w_o[0])),
        "ident": np.eye(128, dtype=f32).astype(bf16),
    }
    for i in range(2):
        shared[f"w_ada{i}"] = tile_b(A(w_ada[i])); shared[f"b_ada{i}"] = row(b_ada[i])
        shared[f"g_mix{i}"] = row(g_mix[i]); shared[f"g_ffn{i}"] = row(g_ffn[i])
        shared[f"w_gate{i}"] = tile_b(A(w_gate[i])); shared[f"w_up{i}"] = tile_b(A(w_up[i])); shared[f"w_down{i}"] = tile_b(A(w_down[i]))
    k = np.arange(SEQ, dtype=np.int64)
    ang = 2.0 * np.pi * ((k[:, None] * k[None, :]) % SEQ) / SEQ
    Cn = np.cos(ang) / np.sqrt(SEQ); Sn = np.sin(ang) / np.sqrt(SEQ)
    kc = np.arange(CTX, dtype=np.int64)
    angc = 2.0 * np.pi * ((kc[:, None] * kc[None, :]) % CTX) / CTX
    Cc = np.cos(angc) / np.sqrt(CTX); Sc = np.sin(angc) / np.sqrt(CTX)
    j = np.arange(GD, dtype=np.int64)
    angm = 2.0 * np.pi * ((j[:, None] * j[None, :]) % GD) / GD
    shared["bch"] = (np.concatenate([np.cos(angm), -np.sin(angm)], 0) / np.sqrt(GD)).astype(f32).astype(bf16)
    rows_n = SEQ // GW
    rr = np.repeat(np.arange(rows_n, dtype=f32), GW); cc = np.tile(np.arange(GW, dtype=f32), rows_n)
    nf = ROPE // 4
    inv = (10000.0 ** (-np.arange(nf, dtype=f32) / nf)).astype(f32)
    angr = np.concatenate([rr[:, None] * inv, cc[:, None] * inv], -1).astype(f32)
    cs, sn = np.cos(angr).astype(f32), np.sin(angr).astype(f32)
    COS = np.concatenate([cs[:, :nf], cs[:, :nf], cs[:, nf:], cs[:, nf:]], -1)
    SIN = np.concatenate([-sn[:, :nf], sn[:, :nf], -sn[:, nf:], sn[:, nf:]], -1)
    x = np.asarray(x, f32); ctx = np.asarray(ctx, f32)
    ins = []
    apos_cache = {}
    for core in range(NCORES):
        b, q = core // 4, core % 4
        perm = np.concatenate([np.arange(q * TQ, (q + 1) * TQ)] + [np.arange(r * TQ, (r + 1) * TQ) for r in range(4) if r != q])
        d = dict(shared)
        d["xin"] = np.ascontiguousarray(np.concatenate([x[b][perm], ctx[b]], 0))
        cvv = np.zeros((128, D), f32); cvv[0] = np.asarray(c, f32)[b]; cvv[1] = np.asarray(c_ctx, f32)
        d["cv"] = cvv
        if q not in apos_cache:
            Ap = np.zeros((2 * T, T), f32)
            Ap[0:SEQ, 0:SEQ] = Cn[np.ix_(perm, perm)]
            Ap[SEQ:2 * SEQ, 0:SEQ] = Sn[np.ix_(perm, perm)]
            Ap[2 * SEQ:2 * SEQ + CTX, SEQ:] = Cc
            Ap[2 * SEQ + CTX:, SEQ:] = Sc
            apos_cache[q] = tile_at(Ap.astype(bf16))
        d["apos"] = apos_cache[q]
        d["cosq"] = np.ascontiguousarray(np.tile(COS[perm[:TQ]], (1, H)))
        d["sinq"] = np.ascontiguousarray(np.tile(SIN[perm[:TQ]], (1, H)))
        d["cosk"] = np.ascontiguousarray(np.concatenate([COS[perm], np.ones((CTX, ROPE), f32)], 0))
        d["sink"] = np.ascontiguousarray(np.concatenate([SIN[perm], np.zeros((CTX, ROPE), f32)], 0))
        ins.append(d)
    return ins


_prog_cache = {}


def run_fused(cfg, **inputs):
    key = tuple(sorted(cfg.items()))
    if key not in _prog_cache:
        _prog_cache[key] = build_fused(cfg)
    nc = _prog_cache[key]
    ins = host_inputs(cfg, **inputs)
    res = run_bass_kernel_spmd(nc, ins, core_ids=list(range(NCORES)))
    SEQ, D = cfg["SEQ"], cfg["D"]
    TQ = SEQ // 4
    out = np.zeros((cfg["BATCH"], SEQ, D), np.float32)
    for core in range(NCORES):
        b, q = core // 4, core % 4
        out[b, q * TQ:(q + 1) * TQ] = res.results[core]["out"]
    return out


def kernel(x, c, ctx, c_ctx, w_ada, b_ada, g_mix, g_ffn, fourier_w_out, mla_w_a, mla_g_q, mla_g_kv,
           mla_w_uq, mla_w_ukv, mla_w_o, w_gate, w_up, w_down, g_final):
    return run_fused(CFG_FULL, x=x, c=c, ctx=ctx, c_ctx=c_ctx, w_ada=w_ada, b_ada=b_ada, g_mix=g_mix, g_ffn=g_ffn,
                     fourier_w_out=fourier_w_out, mla_w_a=mla_w_a, mla_g_q=mla_g_q, mla_g_kv=mla_g_kv,
                     mla_w_uq=mla_w_uq, mla_w_ukv=mla_w_ukv, mla_w_o=mla_w_o, w_gate=w_gate, w_up=w_up,
                     w_down=w_down, g_final=g_final)
```
